# Optimizing a Trainium2 kernel written in Bass

```python
import jax, jax.numpy as jnp
from jax import lax
import numpy as np

D_MODEL = 1024
BATCH = 8
SEQ = 2048
DEPTH = 2

GRID_W = 64
CTX_LEN = 256
D_MIX = D_MODEL
D_RET = D_MIX // 2
D_RWKV = D_MIX - D_RET
HEAD_DIM = 64
N_RET_HEADS = D_RET // HEAD_DIM
N_RWKV_HEADS = D_RWKV // HEAD_DIM
RET_CHUNK = 128
LORA_W = 64
LORA_A = 64
SHIFT_K = 3
D_SHIFT = 3 * D_RWKV + LORA_W + LORA_A
D_IN = 4 * D_RET + D_SHIFT + D_RWKV
ROPE_BASE = 10000.0
NORM_EPS = 1e-6
RWKV_GN_EPS = 64e-5
IN_SPLITS = [D_RET, 2 * D_RET, 3 * D_RET, 4 * D_RET, 4 * D_RET + D_SHIFT]
RWKV_SPLITS = [D_RWKV, 2 * D_RWKV, 3 * D_RWKV, 3 * D_RWKV + LORA_W]

kernel_name = 'hymba_retention_rwkv7_prefix_dit'


def rms_norm(x, w):
    xf = x.astype(jnp.float32)
    return xf * lax.rsqrt(jnp.mean(xf * xf, axis=-1, keepdims=True) + NORM_EPS) * w


def adaln(cond, w_mod, b_mod):
    m = jax.nn.silu(cond.astype(jnp.float32)) @ w_mod + b_mod
    return jnp.split(m, 3, axis=-1)


def to_heads(t, n_heads):
    b, l, _ = t.shape
    return t.astype(jnp.float32).reshape(b, l, n_heads, HEAD_DIM).transpose(0, 2, 1, 3)


def rope_1d(xh, pos):
    nf = xh.shape[-1] // 2
    inv = ROPE_BASE ** (-jnp.arange(nf, dtype=jnp.float32) / nf)
    ang = pos[:, None] * inv[None, :]
    cos, sin = jnp.cos(ang), jnp.sin(ang)
    x1, x2 = xh[..., :nf], xh[..., nf:]
    return jnp.concatenate([x1 * cos - x2 * sin, x1 * sin + x2 * cos], axis=-1)


def rope_2d(x, row_pos, col_pos):
    half = x.shape[-1] // 2
    return jnp.concatenate([rope_1d(x[..., :half], row_pos), rope_1d(x[..., half:], col_pos)], axis=-1)


def retention_scan(q, k, v, log_gamma, s0):
    b, h, l, dh = q.shape
    n_chunks = l // RET_CHUNK
    to_chunks = lambda t: t.reshape(b, h, n_chunks, RET_CHUNK, dh).transpose(2, 0, 1, 3, 4)
    lg = log_gamma.astype(jnp.float32)[:, None]
    idx = jnp.arange(RET_CHUNK, dtype=jnp.float32)
    rel = idx[:, None] - idx[None, :]
    decay_mask = jnp.where(rel >= 0, jnp.exp(lg[:, :, None] * jnp.maximum(rel, 0.0)), 0.0)
    q_decay = jnp.exp(lg * (idx + 1.0))[:, :, None]
    k_decay = jnp.exp(lg * (RET_CHUNK - 1.0 - idx))[:, :, None]
    chunk_decay = jnp.exp(lg * RET_CHUNK)[:, :, None]

    def step(s, qkv):
        qc, kc, vc = qkv
        scores = jnp.einsum('bhid,bhjd->bhij', qc, kc) * decay_mask
        inner = jnp.einsum('bhij,bhjd->bhid', scores, vc)
        cross = jnp.einsum('bhid,bhde->bhie', qc * q_decay, s)
        s_new = s * chunk_decay + jnp.einsum('bhjd,bhje->bhde', kc * k_decay, vc)
        return s_new, inner + cross

    s_fin, o = lax.scan(step, s0, (to_chunks(q), to_chunks(k), to_chunks(v)))
    return o.transpose(1, 2, 0, 3, 4).reshape(b, h, l, dh), s_fin


def head_norm_merge(o, norm_w):
    o = o * lax.rsqrt(jnp.mean(o * o, axis=-1, keepdims=True) + NORM_EPS)
    b, h, l, dh = o.shape
    return o.transpose(0, 2, 1, 3).reshape(b, l, h * dh) * norm_w


def retention_branch(ql, kl, vl, qc, kc, vc, row_pos, col_pos, log_gamma, norm_w):
    hd = lambda t: to_heads(t, N_RET_HEADS)
    scale = HEAD_DIM ** -0.5
    ql = rope_2d(hd(ql), row_pos, col_pos)
    kl = rope_2d(hd(kl), row_pos, col_pos) * scale
    vl = hd(vl)
    qc, kc, vc = hd(qc), hd(kc) * scale, hd(vc)
    s0 = jnp.zeros((ql.shape[0], N_RET_HEADS, HEAD_DIM, HEAD_DIM), jnp.float32)
    fl = lambda t: jnp.flip(t, axis=2)
    oc_f, sc_f = retention_scan(qc, kc, vc, log_gamma[0], s0)
    ol_f, _ = retention_scan(ql, kl, vl, log_gamma[0], sc_f)
    oc_b, sc_b = retention_scan(fl(qc), fl(kc), fl(vc), log_gamma[1], s0)
    ol_b, _ = retention_scan(fl(ql), fl(kl), fl(vl), log_gamma[1], sc_b)
    out_l = head_norm_merge(ol_f + fl(ol_b), norm_w)
    out_c = head_norm_merge(oc_f + fl(oc_b), norm_w)
    return out_l, out_c


def centred_conv(x, w):
    k_size = w.shape[0]
    pad = k_size // 2
    l = x.shape[1]
    xp = jnp.pad(x, ((0, 0), (pad, pad), (0, 0)))
    return sum(xp[:, i:i + l] * w[i] for i in range(k_size))


def rwkv7_scan(r, w, k, v, kap, a, s0, reverse):
    xs = tuple(jnp.swapaxes(t, 0, 1) for t in (r, w, k, v, kap, a))

    def step(s, inp):
        r_t, w_t, k_t, v_t, kap_t, a_t = inp
        s_kap = jnp.einsum('bhvk,bhk->bhv', s, kap_t)
        s = (s * w_t[:, :, None, :]
             - s_kap[..., None] * (kap_t * a_t)[:, :, None, :]
             + v_t[..., None] * k_t[:, :, None, :])
        return s, jnp.einsum('bhvk,bhk->bhv', s, r_t)

    s_fin, y = lax.scan(step, s0, xs, reverse=reverse)
    return jnp.swapaxes(y, 0, 1), s_fin


def rwkv_branch(pl, pc, shift_w, w0, w2, a0, a2, k_k, k_a, r_k, ln_w, ln_b):
    def features(p):
        h = centred_conv(p.astype(jnp.float32), shift_w)
        r, k, v, xw, xa = jnp.split(h, RWKV_SPLITS, axis=-1)
        b, l, _ = r.shape
        hs = lambda t: t.reshape(b, l, N_RWKV_HEADS, HEAD_DIM)
        kk = hs(k * k_k)
        kk = kk / jnp.maximum(jnp.sqrt(jnp.sum(kk * kk, axis=-1, keepdims=True)), 1e-12)
        dirs = []
        for d in range(2):
            w_log = -jax.nn.softplus(-(w0[d] + jnp.tanh(xw) @ w2[d])) - 0.5
            decay = jnp.exp(-jnp.exp(w_log))
            a = jax.nn.sigmoid(a0[d] + xa @ a2[d])
            k_mod = k * (1.0 + (a - 1.0) * k_a)
            dirs.append((hs(r), hs(decay), hs(k_mod), hs(v), kk, hs(a)))
        bonus = jnp.sum(hs(r) * hs(k) * r_k, axis=-1, keepdims=True) * hs(v)
        return dirs, bonus

    def readout(y, bonus):
        mu = jnp.mean(y, axis=-1, keepdims=True)
        var = jnp.mean(jnp.square(y - mu), axis=-1, keepdims=True)
        y = (y - mu) * lax.rsqrt(var + RWKV_GN_EPS) * ln_w.reshape(N_RWKV_HEADS, HEAD_DIM) \
            + ln_b.reshape(N_RWKV_HEADS, HEAD_DIM) + bonus
        b, l = y.shape[:2]
        return y.reshape(b, l, D_RWKV)

    dirs_c, bonus_c = features(pc)
    dirs_l, bonus_l = features(pl)
    s0 = jnp.zeros((pl.shape[0], N_RWKV_HEADS, HEAD_DIM, HEAD_DIM), jnp.float32)
    y_c, y_l = 0.0, 0.0
    for d in range(2):
        rev = d == 1
        yc, sc = rwkv7_scan(*dirs_c[d], s0, rev)
        yl, _ = rwkv7_scan(*dirs_l[d], sc, rev)
        y_c = y_c + yc
        y_l = y_l + yl
    return readout(y_l, bonus_l), readout(y_c, bonus_c)


def setup_inputs(seed: int = 0) -> dict:
    key = jax.random.key(seed)
    ks = jax.random.split(key, 24)
    f32 = jnp.float32
    nrm = lambda k, shape, s: jax.random.normal(k, shape, f32) * s
    x = nrm(ks[0], (BATCH, SEQ, D_MODEL), 1.0)
    c = nrm(ks[1], (BATCH, D_MODEL), 1.0)
    ctx = nrm(ks[2], (BATCH, CTX_LEN, D_MODEL), 1.0)
    c_ctx = nrm(ks[3], (D_MODEL,), 1.0)
    norm_w = 1.0 + nrm(ks[4], (DEPTH, D_MODEL), 0.05)
    w_mod = nrm(ks[5], (DEPTH, D_MODEL, 3 * D_MODEL), 0.5 * D_MODEL ** -0.5)
    b_mod = nrm(ks[6], (DEPTH, 3 * D_MODEL), 0.02)
    w_in = nrm(ks[7], (DEPTH, D_MODEL, D_IN), D_MODEL ** -0.5)
    base_lg = jnp.asarray(np.log(1.0 - 2.0 ** (-5.0 - np.arange(N_RET_HEADS))).astype(np.float32))
    ret_log_gamma = base_lg * jnp.exp(nrm(ks[8], (DEPTH, 2, N_RET_HEADS), 0.1))
    ret_norm_w = 1.0 + nrm(ks[9], (DEPTH, D_RET), 0.05)
    base_shift = jnp.array([0.25, 0.5, 0.25], f32)[:, None]
    rwkv_shift_w = base_shift + nrm(ks[10], (DEPTH, SHIFT_K, D_SHIFT), 0.1)
    rwkv_w0 = jnp.linspace(-4.0, 1.0, D_RWKV, dtype=f32) + nrm(ks[11], (DEPTH, 2, D_RWKV), 0.2)
    rwkv_w2 = nrm(ks[12], (DEPTH, 2, LORA_W, D_RWKV), 0.5 * LORA_W ** -0.5)
    rwkv_a0 = nrm(ks[13], (DEPTH, 2, D_RWKV), 0.1)
    rwkv_a2 = nrm(ks[14], (DEPTH, 2, LORA_A, D_RWKV), 0.5 * LORA_A ** -0.5)
    rwkv_k_k = 0.85 + nrm(ks[15], (DEPTH, D_RWKV), 0.05)
    rwkv_k_a = 1.0 + nrm(ks[16], (DEPTH, D_RWKV), 0.05)
    rwkv_r_k = nrm(ks[17], (DEPTH, N_RWKV_HEADS, HEAD_DIM), 0.1)
    rwkv_ln_w = 1.0 + nrm(ks[18], (DEPTH, D_RWKV), 0.05)
    rwkv_ln_b = nrm(ks[19], (DEPTH, D_RWKV), 0.02)
    w_out = nrm(ks[20], (DEPTH, D_MIX, D_MODEL), D_MIX ** -0.5)
    final_norm_w = 1.0 + nrm(ks[21], (D_MODEL,), 0.05)
    return {'x': x, 'c': c, 'ctx': ctx, 'c_ctx': c_ctx, 'norm_w': norm_w, 'w_mod': w_mod,
            'b_mod': b_mod, 'w_in': w_in, 'ret_log_gamma': ret_log_gamma, 'ret_norm_w': ret_norm_w,
            'rwkv_shift_w': rwkv_shift_w, 'rwkv_w0': rwkv_w0, 'rwkv_w2': rwkv_w2, 'rwkv_a0': rwkv_a0,
            'rwkv_a2': rwkv_a2, 'rwkv_k_k': rwkv_k_k, 'rwkv_k_a': rwkv_k_a, 'rwkv_r_k': rwkv_r_k,
            'rwkv_ln_w': rwkv_ln_w, 'rwkv_ln_b': rwkv_ln_b, 'w_out': w_out, 'final_norm_w': final_norm_w}


def reference(x, c, ctx, c_ctx, norm_w, w_mod, b_mod, w_in, ret_log_gamma, ret_norm_w,
              rwkv_shift_w, rwkv_w0, rwkv_w2, rwkv_a0, rwkv_a2, rwkv_k_k, rwkv_k_a, rwkv_r_k,
              rwkv_ln_w, rwkv_ln_b, w_out, final_norm_w):
    f32 = jnp.float32
    seq_len = x.shape[1]
    ROWS = seq_len // GRID_W
    row_pos = jnp.repeat(jnp.arange(ROWS, dtype=f32), GRID_W)
    col_pos = jnp.tile(jnp.arange(GRID_W, dtype=f32), ROWS)
    xl, xc = x, ctx
    for layer in range(DEPTH):
        sh_l, sc_l, g_l = adaln(c, w_mod[layer], b_mod[layer])
        sh_c, sc_c, g_c = adaln(c_ctx, w_mod[layer], b_mod[layer])
        hl = rms_norm(xl, norm_w[layer]) * (1.0 + sc_l[:, None]) + sh_l[:, None]
        hc = rms_norm(xc, norm_w[layer]) * (1.0 + sc_c) + sh_c
        pl = hl @ w_in[layer]
        pc = hc @ w_in[layer]
        ql, kl, vl, gr_l, rw_l, gw_l = jnp.split(pl, IN_SPLITS, axis=-1)
        qc, kc, vc, gr_c, rw_c, gw_c = jnp.split(pc, IN_SPLITS, axis=-1)
        ret_l, ret_c = retention_branch(ql, kl, vl, qc, kc, vc, row_pos, col_pos,
                                        ret_log_gamma[layer], ret_norm_w[layer])
        rwo_l, rwo_c = rwkv_branch(rw_l, rw_c, rwkv_shift_w[layer], rwkv_w0[layer], rwkv_w2[layer],
                                   rwkv_a0[layer], rwkv_a2[layer], rwkv_k_k[layer], rwkv_k_a[layer],
                                   rwkv_r_k[layer], rwkv_ln_w[layer], rwkv_ln_b[layer])
        mix_l = jnp.concatenate([ret_l * jax.nn.silu(gr_l), rwo_l * jax.nn.silu(gw_l)], axis=-1) @ w_out[layer]
        xl = xl + g_l[:, None] * mix_l
        if layer < DEPTH - 1:
            mix_c = jnp.concatenate([ret_c * jax.nn.silu(gr_c), rwo_c * jax.nn.silu(gw_c)], axis=-1) @ w_out[layer]
            xc = xc + g_c * mix_c
    return rms_norm(xl, final_norm_w)
```

```python
import contextlib
import numpy as np
import concourse.bass as bass
import concourse.mybir as mybir
from concourse.bass_utils import run_bass_kernel_spmd

F32 = mybir.dt.float32
BF16 = mybir.dt.bfloat16
AF = mybir.ActivationFunctionType
ALU = mybir.AluOpType

NT, NC_, NL, DM = 2304, 256, 2048, 1024
TBLK = [(0, 256), (256, 512), (768, 512), (1280, 512), (1792, 512)]
DEPTH = 2
EPS = 1e-6
GN_EPS = 64e-5
ENGF = {'pe': 'tensor', 'act': 'scalar', 'dve': 'vector', 'pool': 'gpsimd', 'sp': 'sync'}

DEBUG = {}


class Sched:
    def __init__(self, nc, stack, ndma=40):
        self.nc = nc
        self.nblk = 0
        self.esem = {e: stack.enter_context(nc.semaphore(f"sem_e_{e}")) for e in ENGF}
        self.ecount = {e: 0 for e in ENGF}
        self.dpool = [stack.enter_context(nc.semaphore(f"sem_d_{i}")) for i in range(ndma)]
        self.dtotal = [0] * ndma
        self.rec = None
        self.keymap = None
        self.reset()

    def reset(self):
        self.ops = []
        self.last_w = {}
        self.readers = {}
        self.eseq = {}

    def record(self, keymap):
        self.rec = []
        self.keymap = keymap
        return self.rec

    def end_record(self):
        self.rec = None
        self.keymap = None

    def merge_sched(self, streams):
        its = [list(x) for x in streams]
        ptr = [0] * len(its)
        E, Wt, Rt = {}, {}, {}
        LAT = DEBUG.get('lat', 16.0)

        def start_of(o):
            eng, fn, r, w, dk, ss, cost = o
            t = E.get(eng, 0.0)
            for k in list(r) + list(w):
                if k in Wt:
                    t = max(t, Wt[k][0] + (LAT if Wt[k][1] != eng else 0.0))
            for k in w:
                if k in Rt:
                    t = max(t, Rt[k])
            return t
        while True:
            best, bt = None, None
            for i, x in enumerate(its):
                if ptr[i] < len(x):
                    t = start_of(x[ptr[i]])
                    if bt is None or t < bt:
                        best, bt = i, t
            if best is None:
                break
            o = its[best][ptr[best]]
            ptr[best] += 1
            eng, fn, r, w, dk, ss, cost = o
            end = bt + cost
            E[eng] = end
            for k in w:
                Wt[k] = (end, eng)
                Rt.pop(k, None)
            for k in r:
                Rt[k] = max(Rt.get(k, 0.0), end)
            self._add(*o)

    def merge(self, streams, offsets=None):
        its = [list(x) for x in streams]
        offsets = offsets or [0] * len(its)
        n = max((len(x) + o for x, o in zip(its, offsets)), default=0)
        for i in range(n):
            for x, o in zip(its, offsets):
                if 0 <= i - o < len(x):
                    self._add(*x[i - o])

    def _add(self, eng, fn, r, w, dma_key=None, sync_same=False, cost=0.3):
        if getattr(self, 'rec', None) is not None:
            km = self.keymap
            self.rec.append((eng, fn, [km(k) for k in r], [km(k) for k in w], dma_key, sync_same, cost))
            return
        idx = len(self.ops)
        deps = set()
        raw = set()
        for k in r:
            if k in self.last_w:
                deps.add(self.last_w[k])
                raw.add(self.last_w[k])
        for k in w:
            if k in self.last_w:
                deps.add(self.last_w[k])
            for x in self.readers.get(k, {}).values():
                deps.update(x)
        for k in w:
            self.last_w[k] = idx
            self.readers[k] = {}
        for k in r:
            rd = self.readers.setdefault(k, {})
            if dma_key is None:
                rd[eng] = [idx]
            else:
                rd.setdefault('dma', []).append(idx)
        deps.discard(idx)
        self.eseq[eng] = self.eseq.get(eng, 0) + 1
        self.ops.append(dict(eng=eng, fn=fn, deps=deps, dma=dma_key, need=False, cnt=0, ss=sync_same, raw=raw,
                             eseq=self.eseq[eng]))

    @staticmethod
    def _same_sync(a, o, d):
        if a['ss']:
            return True
        return a['eng'] != 'pe' and d in o['raw'] and (o['eseq'] - a['eseq']) <= 4

    def op(self, eng, fn, r=(), w=(), sync_same=False, cost=0.3):
        self._add(eng, fn, r, w, sync_same=sync_same, cost=cost)

    def dma(self, q, out, in_, r=(), w=(), key=None):
        self._add(q, lambda e: e.dma_start(out=out, in_=in_), r, w, dma_key=key)

    def flush(self, name=None):
        ops = self.ops
        if not ops:
            return
        nc = self.nc
        self.nblk += 1
        name = f"{name or 'blk'}{self.nblk}"
        for o in ops:
            for d in o['deps']:
                a = ops[d]
                if a['dma'] is not None:
                    continue
                if a['eng'] == o['eng'] and o['dma'] is None and not self._same_sync(a, o, d):
                    continue
                a['need'] = True
        cnt, dcnt, didx = dict(self.ecount), {}, {}
        for o in ops:
            if o['dma'] is not None:
                k = o['dma']
                if k not in didx:
                    didx[k] = len(didx)
                    assert didx[k] < len(self.dpool), "too many DMA keys in one block"
                    dcnt[k] = self.dtotal[didx[k]]
                dcnt[k] += 16
                o['cnt'] = dcnt[k]
            elif o['need']:
                cnt[o['eng']] += 1
                o['cnt'] = cnt[o['eng']]
        for k, i in didx.items():
            self.dtotal[i] = dcnt[k]
        self.ecount = cnt
        engs = sorted(set(o['eng'] for o in ops))
        with contextlib.ExitStack() as st:
            esem = self.esem
            dsem = {k: self.dpool[i] for k, i in didx.items()}
            block = st.enter_context(nc.Block(name))

            def emit(e_name):
                def body(e):
                    waited = {}
                    my_dma = {}
                    for o in ops:
                        if o['eng'] != e_name:
                            continue
                        need = {}
                        for d in o['deps']:
                            a = ops[d]
                            if a['dma'] is not None:
                                key = ('d', a['dma'])
                            elif a['eng'] == e_name and o['dma'] is None and not self._same_sync(a, o, d):
                                continue
                            else:
                                key = ('e', a['eng'])
                            if a['cnt'] > need.get(key, 0):
                                need[key] = a['cnt']
                        for key, val in need.items():
                            if waited.get(key, 0) >= val:
                                continue
                            waited[key] = val
                            sem = dsem[key[1]] if key[0] == 'd' else esem[key[1]]
                            e.wait_ge(sem, val)
                        ins = o['fn'](e)
                        if o['dma'] is not None:
                            ins.then_inc(dsem[o['dma']], 16)
                            my_dma[o['dma']] = o['cnt']
                        elif o['need']:
                            ins.then_inc(esem[e_name], 1)
                    for k, v in my_dma.items():
                        if waited.get(('d', k), 0) < v:
                            e.wait_ge(dsem[k], v)
                return body

            for e_name in engs:
                getattr(block, ENGF[e_name])(emit(e_name))
        self.reset()


def _fs(ap):
    try:
        return float(ap.free_size())
    except Exception:
        return 256.0


def _ec(ap):
    return 0.07 + _fs(ap) / 960.0


def mm(S, out, lhsT, rhs, start=True, stop=True, tp=None, r=(), w=()):
    kw = {} if tp is None else {'tile_position': tp}
    S.op('pe', lambda e: e.matmul(out, lhsT, rhs, start=start, stop=stop, **kw), r, w, cost=_fs(rhs) * 4 / 2400.0 + 0.012)


def tr(S, out, in_, ident, r=(), w=()):
    S.op('pe', lambda e: e.transpose(out, in_, ident), r, w)


def act(S, out, in_, func, r=(), w=(), scale=None, bias=None):
    kw = {}
    if scale is not None:
        kw['scale'] = scale
    if bias is not None:
        kw['bias'] = bias
    S.op('act', lambda e: e.activation(out=out, in_=in_, func=func, **kw), r, w, cost=_ec(out) + 0.15)


def tt(S, out, in0, in1, op, r=(), w=(), eng='dve'):
    S.op(eng, lambda e: e.tensor_tensor(out=out, in0=in0, in1=in1, op=op), r, w, cost=_ec(out))


def ts(S, out, in0, s1, s2, op0, op1=None, r=(), w=(), eng='dve'):
    if op1 is None:
        S.op(eng, lambda e: e.tensor_scalar(out=out, in0=in0, scalar1=s1, scalar2=None, op0=op0), r, w, cost=_ec(out))
    else:
        S.op(eng, lambda e: e.tensor_scalar(out=out, in0=in0, scalar1=s1, scalar2=s2, op0=op0, op1=op1), r, w, cost=_ec(out))


def stt(S, out, in0, scalar, in1, op0, op1, r=(), w=(), eng='dve'):
    S.op(eng, lambda e: e.scalar_tensor_tensor(out=out, in0=in0, scalar=scalar, in1=in1, op0=op0, op1=op1), r, w, cost=_ec(out))


def cp(S, out, in_, r=(), w=(), eng='dve'):
    if eng == 'act':
        S.op('act', lambda e: e.activation(out=out, in_=in_, func=AF.Copy), r, w, cost=_ec(out) + 0.15)
    else:
        S.op(eng, lambda e: e.tensor_copy(out=out, in_=in_), r, w, cost=_ec(out))


def recip(S, out, in_, r=(), w=()):
    S.op('dve', lambda e: e.reciprocal(out=out, in_=in_), r, w)


def mset(S, ap, val, r=(), w=(), eng='dve'):
    S.op(eng, lambda e: e.memset(ap, val), r, w)


def hsl(h):
    return slice(64 * h, 64 * h + 64)


def _consts():
    cols = {}
    parts = []
    off = [0]

    def add(name, a):
        a = np.ascontiguousarray(a, dtype=np.float32).reshape(128, -1)
        cols[name] = off[0]
        off[0] += a.shape[1]
        parts.append(a)

    p = np.arange(128)
    f = np.arange(128)
    add('ident', np.eye(128))
    d = p % 64
    partner = np.where((d % 32) < 16, p + 16, p - 16)
    perm = np.zeros((128, 128))
    perm[partner, p] = 1.0
    add('perm', perm)
    bones = np.zeros((128, 128))
    bones[:64, :64] = 1.0
    bones[64:, 64:] = 1.0
    add('bones', bones)
    add('ones', np.ones((128, 128)))
    add('jcol', p.astype(np.float64)[:, None])
    add('jrev', (127 - p).astype(np.float64)[:, None])
    add('irow1', np.broadcast_to((f + 1.0)[None, :], (128, 128)))
    add('irowb', np.broadcast_to((128.0 - f)[None, :], (128, 128)))
    rel = f[None, :] - p[:, None]
    add('relpos', np.maximum(rel, 0))
    add('relneg', np.maximum(-rel, 0))
    add('triu', (rel >= 0))
    add('tril', (rel <= 0))
    pi = (p % 64)[:, None]
    ph = np.arange(64)[None, :]
    su, iu, sl, il = (pi < ph), (pi <= ph), (pi > ph), (pi >= ph)
    add('mab0', np.concatenate([su, iu, su, iu], axis=1))
    add('mab1', np.concatenate([sl, il, sl, il], axis=1))
    add('mn0', np.tile(sl, (1, 4)))
    add('mn1', np.tile(su, (1, 4)))
    add('identrep', np.tile(np.eye(64), (2, 4)))
    cm = np.ones((128, 256))
    cm[:, ::64] = 0.0
    add('cmask', cm)
    t = np.arange(NL)
    rowp = (t // 64).astype(np.float64)
    colp = (t % 64).astype(np.float64)
    inv = 10000.0 ** (-np.arange(16, dtype=np.float64) / 16.0)
    pos = np.where((d < 32)[:, None], rowp[None, :], colp[None, :])
    ang = (pos.astype(np.float32) * inv.astype(np.float32)[d % 16][:, None]).astype(np.float32)
    sgn = np.where((d % 32) < 16, -1.0, 1.0)[:, None]
    rope = np.ascontiguousarray(np.stack([np.cos(ang), np.sin(ang) * sgn], axis=1).astype(np.float32))
    return np.concatenate(parts, axis=1), cols, rope


def _pvec(inp, b):
    cols = {}
    parts = []
    off = [0]

    def add(name, a):
        a = np.ascontiguousarray(a, dtype=np.float32).reshape(128, -1)
        cols[name] = off[0]
        off[0] += a.shape[1]
        parts.append(a)

    add('c', inp['c'][b].reshape(8, 128).T)
    add('cctx', inp['c_ctx'].reshape(8, 128).T)
    add('fnw', inp['final_norm_w'].reshape(8, 128).T)
    for l in range(DEPTH):
        add(f'normw{l}', inp['norm_w'][l].reshape(8, 128).T)
        add(f'bmod{l}', inp['b_mod'][l].reshape(24, 128).T)
        add(f'retnw{l}', inp['ret_norm_w'][l].reshape(4, 128).T)
        add(f'shw{l}', inp['rwkv_shift_w'][l].reshape(3, 13, 128).transpose(2, 0, 1))
        add(f'w0{l}', inp['rwkv_w0'][l].reshape(2, 4, 128).transpose(2, 0, 1))
        add(f'a0{l}', inp['rwkv_a0'][l].reshape(2, 4, 128).transpose(2, 0, 1))
        add(f'kk{l}', inp['rwkv_k_k'][l].reshape(4, 128).T)
        add(f'ka{l}', inp['rwkv_k_a'][l].reshape(4, 128).T)
        add(f'rk{l}', inp['rwkv_r_k'][l].reshape(4, 128).T)
        add(f'lnw{l}', inp['rwkv_ln_w'][l].reshape(4, 128).T)
        add(f'lnb{l}', inp['rwkv_ln_b'][l].reshape(4, 128).T)
        lg = inp['ret_log_gamma'][l]
        add(f'lgcol{l}', np.repeat(lg.reshape(2, 4, 2), 64, axis=2).transpose(2, 0, 1))
        add(f'lgrep{l}', np.broadcast_to(lg.reshape(1, 16), (128, 16)))
    return np.concatenate(parts, axis=1), cols


def build(CC, PC, ncst, npv, dbg=None):
    dbg = dbg or {}
    nlayers = dbg.get('layers', DEPTH)
    nc = bass.Bass("TRN2", target_bir_lowering=False)
    x_d = nc.dram_tensor("x", [NL, DM], F32, kind="ExternalInput").ap()
    ctx_d = nc.dram_tensor("ctx", [NC_, DM], F32, kind="ExternalInput").ap()
    cst_d = nc.dram_tensor("cst", [128, ncst], F32, kind="ExternalInput").ap()
    pv_d = nc.dram_tensor("pv", [128, npv], F32, kind="ExternalInput").ap()
    rope_d = nc.dram_tensor("rope", [128, 2, NL], F32, kind="ExternalInput").ap()
    lora_d = nc.dram_tensor("lora", [128, 8, 512], F32, kind="ExternalInput").ap()
    wmod_d = nc.dram_tensor("w_mod", [DEPTH, DM, 3 * DM], F32, kind="ExternalInput").ap()
    win_d = nc.dram_tensor("w_in", [DEPTH, DM, 4224], F32, kind="ExternalInput").ap()
    wout_d = nc.dram_tensor("w_out", [DEPTH, DM, DM], F32, kind="ExternalInput").ap()
    out_d = nc.dram_tensor("out", [NL, DM], F32, kind="ExternalOutput").ap()
    dk = "ExternalOutput" if dbg.get('dump') else "Internal"
    xt_d = nc.dram_tensor("xt_scr", [128, 8, NT], F32, kind=dk).ap()
    p_d = nc.dram_tensor("p_scr", [33, 128, NT], F32, kind=dk).ap()
    mix_d = nc.dram_tensor("mix_scr", [8, 128, NT], BF16, kind=dk).ap()

    es = contextlib.ExitStack()
    with es:
        S = Sched(nc, es)
        DEBUG['_p_d'] = p_d
        uid = [0]

        def sb(name, shape, dt=F32, stack=es):
            uid[0] += 1
            return stack.enter_context(nc.sbuf_tensor(f"{name}_{uid[0]}", shape, dt))

        cst = sb("cst_sb", [128, ncst])
        pv = sb("pv_sb", [128, npv])
        lora = sb("lora_sb", [128, 8, 512])
        mod = sb("mod_sb", [128, DEPTH, 24, 2])
        Amod = sb("amod_sb", [128, DEPTH, 8, 2])
        PS = [es.enter_context(nc.psum_tensor(f"ps{i}", [128, 512], F32)) for i in range(8)]

        def C(name, n=128, rows=slice(0, 128)):
            return cst[rows, CC[name]:CC[name] + n]

        def P(name, j=0, rows=slice(0, 128)):
            return pv[rows, PC[name] + j:PC[name] + j + 1]

        ident = C('ident')
        psn = [0]

        def nextps():
            psn[0] = (psn[0] + 1) % 8
            return psn[0]

        S.dma('sp', cst[:], cst_d[:, :], w=['cst'], key='cst')
        S.dma('sp', pv[:], pv_d[:, :], w=['pv'], key='pv')
        S.dma('sp', lora[:], lora_d[:, :, :], w=['lora'], key='lora')
        ph0 = contextlib.ExitStack()
        if True:
            ph = ph0
            xin = [sb(f"xin{i}", [128, DM], stack=ph) for i in range(2)]
            xo = [sb(f"xo{i}", [128, 8, 128], stack=ph) for i in range(2)]
            PA_BUFS = (sb("cs", [128, 8, 2], stack=ph0), [sb(f"wm{i}", [128, 8, 128], stack=ph0) for i in range(2)])
            for i in range(18):
                bi = i % 2
                src = ctx_d[i * 128:(i + 1) * 128, :] if i < 2 else x_d[(i - 2) * 128:(i - 1) * 128, :]
                S.dma('sp', xin[bi][:], src, w=[f'xin{bi}'], key=f'xin{bi}')
                for half in range(2):
                    b = nextps()
                    for k in range(4):
                        kc = half * 4 + k
                        tr(S, PS[b][:, k * 128:(k + 1) * 128], xin[bi][:, kc * 128:(kc + 1) * 128], ident,
                           r=[f'xin{bi}', 'cst'], w=[f'ps{b}'])
                    cp(S, xo[bi][:, half * 4:half * 4 + 4, :],
                       PS[b][:, :].rearrange("p (k t) -> p k t", t=128),
                       r=[f'ps{b}'], w=[f'xo{bi}'], eng='act' if half else 'dve')
                S.dma('pool', xt_d[:, :, i * 128:(i + 1) * 128], xo[bi][:], r=[f'xo{bi}'], w=['xt_d'], key=f'xo{bi}')

        with contextlib.ExitStack() as ph:
            cs, wm = PA_BUFS
            act(S, cs[:, :, 0], pv[:, PC['c']:PC['c'] + 8], AF.Silu, r=['pv'], w=['cs'])
            act(S, cs[:, :, 1], pv[:, PC['cctx']:PC['cctx'] + 8], AF.Silu, r=['pv'], w=['cs'])
            it = 0
            for l in range(nlayers):
                for oc in range(24):
                    bi = it % 2
                    it += 1
                    S.dma('sp', wm[bi][:], wmod_d[l, :, oc * 128:(oc + 1) * 128].rearrange("(kc p) n -> p kc n", p=128),
                          w=[f'wm{bi}'], key=f'wm{bi}')
                    b = nextps()
                    for kc in range(8):
                        mm(S, PS[b][:, 0:2], wm[bi][:, kc, :], cs[:, kc, :], start=(kc == 0), stop=(kc == 7),
                           r=[f'wm{bi}', 'cs'], w=[f'ps{b}'])
                    ts(S, mod[:, l, oc, :], PS[b][:, 0:2], P(f'bmod{l}', oc), None, ALU.add,
                       r=[f'ps{b}', 'pv'], w=['mod'])
                for kc in range(8):
                    ts(S, Amod[:, l, kc, :], mod[:, l, 8 + kc, :], 1.0, P(f'normw{l}', kc), ALU.add, ALU.mult,
                       r=['mod', 'pv'], w=['amod'])
            S.flush('pa')
        ph0.close()

        for l in range(nlayers):
            last = (l == DEPTH - 1)
            lastm = last and not dbg.get('nolast')
            with contextlib.ExitStack() as ph:
                hT = sb("hT", [128, 8, NT], BF16, stack=ph)
                with contextlib.ExitStack() as ph2:
                    xb = [sb(f"xb{i}", [128, 8, 512], stack=ph2) for i in range(2)]
                    sq = sb("sq", [128, 8, 512], stack=ph2)
                    rstd = sb("rstd", [128, 512], stack=ph2)
                    tmpf = [sb(f"tmpf{i}", [128, 512], stack=ph2) for i in range(2)]
                    for bi_, (s0, n) in enumerate(TBLK):
                        bi = bi_ % 2
                        si = 1 if s0 == 0 else 0
                        S.dma('sp', xb[bi][:, :, 0:n], xt_d[:, :, s0:s0 + n], r=['xt_d'], w=[f'xb{bi}'], key=f'xb{bi}')
                        act(S, sq[:, :, 0:n], xb[bi][:, :, 0:n], AF.Square, r=[f'xb{bi}'], w=['sq'])
                        b = nextps()
                        for kc in range(8):
                            mm(S, PS[b][:, 0:n], C('ones'), sq[:, kc, 0:n], start=(kc == 0), stop=(kc == 7),
                               r=['sq', 'cst'], w=[f'ps{b}'])
                        ts(S, rstd[:, 0:n], PS[b][:, 0:n], 1.0 / DM, EPS, ALU.mult, ALU.add, r=[f'ps{b}'], w=['rstd'])
                        act(S, rstd[:, 0:n], rstd[:, 0:n], AF.Sqrt, r=['rstd'], w=['rstd'])
                        recip(S, rstd[:, 0:n], rstd[:, 0:n], r=['rstd'], w=['rstd'])
                        for kc in range(8):
                            tb = kc % 2
                            stt(S, tmpf[tb][:, 0:n], xb[bi][:, kc, 0:n], Amod[:, l, kc, si:si + 1], rstd[:, 0:n],
                                ALU.mult, ALU.mult, r=[f'xb{bi}', 'amod', 'rstd'], w=[f'tmpf{tb}'])
                            act(S, hT[:, kc, s0:s0 + n], tmpf[tb][:, 0:n], AF.Identity, bias=mod[:, l, kc, si:si + 1],
                                r=[f'tmpf{tb}', 'mod'], w=['hT'])
                    wst = [sb(f"wst{i}", [128, 8, 128], stack=ph2) for i in range(2)]
                    wbf = [sb(f"wbf{i}", [128, 8, 128], BF16, stack=ph2) for i in range(2)]
                    pst = [sb(f"pst{i}", [128, NT], stack=ph2) for i in range(2)]
                    ev = 0
                    for cc in range(33):
                        bi = cc % 2
                        S.dma('sp', wst[bi][:], win_d[l, :, cc * 128:(cc + 1) * 128].rearrange("(kc p) n -> p kc n", p=128),
                              w=[f'wst{bi}'], key=f'wst{bi}')
                        cp(S, wbf[bi][:], wst[bi][:], r=[f'wst{bi}'], w=[f'wbf{bi}'], eng='dve' if cc % 2 else 'act')
                        for (s0, n) in TBLK:
                            b = nextps()
                            for kc in range(8):
                                mm(S, PS[b][:, 0:n], wbf[bi][:, kc, :], hT[:, kc, s0:s0 + n], start=(kc == 0), stop=(kc == 7),
                                   r=[f'wbf{bi}', 'hT'], w=[f'ps{b}'])
                            ev += 1
                            cp(S, pst[bi][:, s0:s0 + n], PS[b][:, 0:n], r=[f'ps{b}'], w=[f'pst{bi}'],
                               eng='act' if ev % 2 else 'dve')
                        S.dma('pool', p_d[cc, :, :], pst[bi][:], r=[f'pst{bi}'], w=[f'p_d{cc}'], key=f'pst{bi}')
                    S.flush('pc')
            if dbg.get('stop') == 'pc' or (l == 1 and dbg.get('l1stop') == 'pc'):
                break

            if not dbg.get('skip_ret'):
                with contextlib.ExitStack() as ph:
                    qT = sb("qT", [128, NT], stack=ph)
                    kT = sb("kT", [128, NT], stack=ph)
                    vT = sb("vT", [128, NT], stack=ph)
                    gT = sb("gT", [128, NT], stack=ph)
                    oT = sb("oT", [128, NT], stack=ph)
                    ktf = sb("ktf", [128, 18, 128], stack=ph)
                    ktb = sb("ktb", [128, 18, 128], stack=ph)
                    vtok = sb("vtok", [128, 18, 128], stack=ph)
                    Sf = sb("Sf", [128, 18, 64], stack=ph)
                    Sb = sb("Sb", [128, 18, 64], stack=ph)
                    rt1 = sb("rt1", [128, 512], stack=ph)
                    rt2 = sb("rt2", [128, 512], stack=ph)
                    rt1b = sb("rt1b", [128, 512], stack=ph)
                    rt2b = sb("rt2b", [128, 512], stack=ph)
                    kdec = sb("kdec", [128, 4], stack=ph)
                    qfb = sb("qfb", [128, 2, 128], stack=ph)
                    mask = sb("mask", [128, 2, 128], stack=ph)
                    m2 = sb("m2", [128, 128], stack=ph)
                    gam = sb("gam", [128, 2], stack=ph)
                    sm = [sb(f"sm{i}", [128, 256], stack=ph) for i in range(2)]
                    qt = [sb(f"qt{i}", [128, 2, 128], stack=ph) for i in range(2)]
                    mixst = sb("mixst", [128, NT], BF16, stack=ph)
                    qz = [sb(f"qz{i}", [128, NT], stack=ph) for i in range(2)]
                    rope = sb("rope", [128, 2, NL], stack=ph)
                    S.dma('sp', rope[:], rope_d[:, :, :], w=['rope'], key='rope')
                    for p in range(dbg.get('ret_pairs', 4)):
                        for (t_, nm, ci) in ((qT, 'qT', p), (kT, 'kT', 4 + p), (vT, 'vT', 8 + p), (gT, 'gT', 12 + p)):
                            S.dma('sp', t_[:], p_d[ci, :, :], r=[f'p_d{ci}'], w=[nm], key=nm)
                        ts(S, kT[:], kT[:], 0.125, None, ALU.mult, r=['kT'], w=['kT'])
                        ropes = []
                        for si, (t_, nm, ra_, rb_) in enumerate(((qT, 'qT', rt1, rt2), (kT, 'kT', rt1b, rt2b))):
                            ropes.append(S.record(lambda k, si=si: (k + f'_{si}') if k in ('rt1', 'rt2') else k))
                            for i in range(4):
                                s0 = 256 + 512 * i
                                b = 4 * si + i
                                mm(S, PS[b][:, :], C('perm'), t_[:, s0:s0 + 512], r=[nm, 'cst'], w=[f'ps{b}'])
                                tt(S, ra_[:], t_[:, s0:s0 + 512], rope[:, 0, 512 * i:512 * (i + 1)], ALU.mult,
                                   r=[nm, 'rope'], w=['rt1'])
                                tt(S, rb_[:], PS[b][:, :], rope[:, 1, 512 * i:512 * (i + 1)], ALU.mult,
                                   r=[f'ps{b}', 'rope'], w=['rt2'])
                                tt(S, t_[:, s0:s0 + 512], ra_[:], rb_[:], ALU.add, r=['rt1', 'rt2'], w=[nm])
                            S.end_record()
                        S.merge_sched(ropes)
                        if dbg.get('ret_stage', 9) <= 1:
                            continue
                        lgr = PC[f'lgrep{l}']
                        act(S, kdec[:, 0:2], pv[:, lgr + 2 * p:lgr + 2 * p + 2], AF.Exp, scale=C('jrev', 1), r=['pv', 'cst'], w=['kdec'])
                        act(S, kdec[:, 2:4], pv[:, lgr + 8 + 2 * p:lgr + 8 + 2 * p + 2], AF.Exp, scale=C('jcol', 1), r=['pv', 'cst'], w=['kdec'])
                        act(S, qfb[:, 0, :], C('irow1'), AF.Exp, scale=P(f'lgcol{l}', p), r=['pv', 'cst'], w=['qfb'])
                        act(S, qfb[:, 1, :], C('irowb'), AF.Exp, scale=P(f'lgcol{l}', 4 + p), r=['pv', 'cst'], w=['qfb'])
                        act(S, gam[:, 0:1], P(f'lgcol{l}', p), AF.Exp, scale=128.0, r=['pv'], w=['gam'])
                        act(S, gam[:, 1:2], P(f'lgcol{l}', 4 + p), AF.Exp, scale=128.0, r=['pv'], w=['gam'])
                        for h in range(2):
                            act(S, mask[:, h, :], C('relpos'), AF.Exp, scale=pv[:, lgr + 2 * p + h:lgr + 2 * p + h + 1], r=['pv', 'cst'], w=['mask'])
                            tt(S, mask[:, h, :], mask[:, h, :], C('triu'), ALU.mult, r=['mask', 'cst'], w=['mask'])
                            act(S, m2[:], C('relneg'), AF.Exp, scale=pv[:, lgr + 8 + 2 * p + h:lgr + 8 + 2 * p + h + 1], r=['pv', 'cst'], w=['m2'])
                            tt(S, m2[:], m2[:], C('tril'), ALU.mult, r=['m2', 'cst'], w=['m2'])
                            tt(S, mask[:, h, :], mask[:, h, :], m2[:], ALU.add, r=['mask', 'm2'], w=['mask'])
                        if dbg.get('ret_stage', 9) <= 2:
                            continue
                        for c in range(18):
                            b = nextps()
                            tr(S, PS[b][:, 0:128], kT[:, c * 128:(c + 1) * 128], ident, r=['kT', 'cst'], w=[f'ps{b}'])
                            tr(S, PS[b][:, 128:256], vT[:, c * 128:(c + 1) * 128], ident, r=['vT', 'cst'], w=[f'ps{b}'])
                            kps = PS[b][:, 0:128].rearrange("p (h d) -> p h d", d=64)
                            tt(S, ktf[:, c, :].rearrange("p (h d) -> p h d", d=64), kps,
                               kdec[:, 0:2].rearrange("p (h o) -> p h o", o=1).to_broadcast([128, 2, 64]), ALU.mult,
                               r=[f'ps{b}', 'kdec'], w=['ktf'])
                            tt(S, ktb[:, c, :].rearrange("p (h d) -> p h d", d=64), kps,
                               kdec[:, 2:4].rearrange("p (h o) -> p h o", o=1).to_broadcast([128, 2, 64]), ALU.mult,
                               r=[f'ps{b}', 'kdec'], w=['ktb'])
                            cp(S, vtok[:, c, :], PS[b][:, 128:256], r=[f'ps{b}'], w=['vtok'], eng='act')
                        if dbg.get('ret_stage', 9) <= 3:
                            continue
                        mset(S, Sf[:, 0, :], 0.0, w=['Sf'])
                        mset(S, Sb[:, 1, :], 0.0, w=['Sb'])
                        of = list(range(18))
                        ob = [1, 0] + list(range(17, 1, -1))
                        chains = []
                        for (St, snm, kt, knm, order, gi) in ((Sf, 'Sf', ktf, 'ktf', of, 0), (Sb, 'Sb', ktb, 'ktb', ob, 1)):
                            chains.append(S.record(lambda k: k))
                            for i in range(17):
                                c, nx = order[i], order[i + 1]
                                b = 4 * gi + (i % 4)
                                for h in range(2):
                                    mm(S, PS[b][hsl(h), 0:64], kt[:, c, hsl(h)], vtok[:, c, hsl(h)], tp=(0, 64 * h),
                                       r=[knm, 'vtok'], w=[f'ps{b}'])
                                stt(S, St[:, nx, :], St[:, c, :], gam[:, gi:gi + 1], PS[b][:, 0:64], ALU.mult, ALU.add,
                                    r=[snm, 'gam', f'ps{b}'], w=[snm])
                            S.end_record()
                        S.merge(chains)
                        if dbg.get('ret_stage', 9) <= 4:
                            continue
                        for h in range(2):
                            mset(S, qz[h][hsl(1 - h), :], 0.0, w=[f'qz{h}'])
                            cp(S, qz[h][hsl(h), :], qT[hsl(h), :], r=['qT'], w=[f'qz{h}'], eng='act' if h else 'dve')
                        for c in range(2 if lastm else 0, 18):
                            cs_ = slice(c * 128, (c + 1) * 128)
                            bi = c % 2
                            b = nextps()
                            for h in range(2):
                                mm(S, PS[b][:, h * 128:(h + 1) * 128], kT[:, cs_], qz[h][:, cs_], r=['kT', f'qz{h}'], w=[f'ps{b}'])
                            tt(S, sm[bi][:], PS[b][:, 0:256], mask[:].rearrange("p h i -> p (h i)"), ALU.mult,
                               r=[f'ps{b}', 'mask'], w=[f'sm{bi}'])
                            tt(S, qt[bi][:], qfb[:], qT[:, cs_].rearrange("p (o i) -> p o i", o=1).to_broadcast([128, 2, 128]), ALU.mult,
                               r=['qfb', 'qT'], w=[f'qt{bi}'])
                            b2 = nextps()
                            for h in range(2):
                                mm(S, PS[b2][hsl(h), 0:128], vtok[:, c, hsl(h)], sm[bi][:, h * 128:(h + 1) * 128], start=True, stop=True,
                                   tp=(0, 64 * h), r=['vtok', f'sm{bi}'], w=[f'ps{b2}'])
                            b3 = nextps()
                            for h in range(2):
                                mm(S, PS[b3][hsl(h), 0:128], Sf[hsl(h), c, :], qt[bi][hsl(h), 0, :], start=True, stop=False,
                                   tp=(64 * h, 64 * h), r=['Sf', f'qt{bi}'], w=[f'ps{b3}'])
                                mm(S, PS[b3][hsl(h), 0:128], Sb[hsl(h), c, :], qt[bi][hsl(h), 1, :], start=False, stop=True,
                                   tp=(64 * h, 64 * h), r=['Sb', f'qt{bi}'], w=[f'ps{b3}'])
                            cp(S, oT[:, cs_], PS[b2][:, 0:128], r=[f'ps{b2}'], w=['oT'], eng='act')
                            tt(S, oT[:, cs_], oT[:, cs_], PS[b3][:, 0:128], ALU.add, r=['oT', f'ps{b3}'], w=['oT'])
                        if dbg.get('ret_stage', 9) <= 5:
                            continue
                        act(S, gT[:], gT[:], AF.Silu, r=['gT'], w=['gT'])
                        for (s0, n) in TBLK:
                            if lastm and s0 == 0:
                                continue
                            act(S, rt1[:, 0:n], oT[:, s0:s0 + n], AF.Square, r=['oT'], w=['rt1'])
                            b = nextps()
                            mm(S, PS[b][:, 0:n], C('bones'), rt1[:, 0:n], r=['rt1', 'cst'], w=[f'ps{b}'])
                            ts(S, rt2[:, 0:n], PS[b][:, 0:n], 1.0 / 64, EPS, ALU.mult, ALU.add, r=[f'ps{b}'], w=['rt2'])
                            act(S, rt2[:, 0:n], rt2[:, 0:n], AF.Sqrt, r=['rt2'], w=['rt2'])
                            recip(S, rt2[:, 0:n], rt2[:, 0:n], r=['rt2'], w=['rt2'])
                            tt(S, rt1[:, 0:n], oT[:, s0:s0 + n], rt2[:, 0:n], ALU.mult, r=['oT', 'rt2'], w=['rt1'])
                            stt(S, mixst[:, s0:s0 + n], rt1[:, 0:n], P(f'retnw{l}', p), gT[:, s0:s0 + n], ALU.mult, ALU.mult,
                                r=['rt1', 'pv', 'gT'], w=['mixst'])
                        lo = 256 if lastm else 0
                        S.dma('pool', mix_d[p, :, lo:NT], mixst[:, lo:NT], r=['mixst'], w=[f'mix_d{p}'], key='mixst')
                    S.flush('pd')

            if l == 1 and dbg.get('l1stop') == 'pd':
                break
            if not dbg.get('skip_rwkv'):
                rwkv_phase(nc, S, sb, PS, nextps, cst, pv, lora, CC, PC, C, P, p_d, mix_d, l, lastm, dbg)

            if dbg.get('stop') == 'mix' or (l == 1 and dbg.get('l1stop') == 'pe'):
                break
            with contextlib.ExitStack() as ph:
                wof = sb("wof", [128, 8, DM], stack=ph)
                wob = sb("wob", [128, 8, DM], BF16, stack=ph)
                mixb = [sb(f"mixb{i}", [128, 8, 512], BF16, stack=ph) for i in range(2)]
                xb = [sb(f"xb{i}", [128, 8, 512], stack=ph) for i in range(2)]
                xn = [sb(f"xn{i}", [128, 8, 512], stack=ph) for i in range(2)]
                S.dma('sp', wof[:], wout_d[l, :, :].rearrange("(kc p) n -> p kc n", p=128), w=['wof'], key='wof')
                for kc in range(8):
                    cp(S, wob[:, kc, :], wof[:, kc, :], r=['wof'], w=['wob'], eng='act' if kc % 2 else 'dve')
                if last:
                    sq = sb("sq", [128, 8, 512], stack=ph)
                    rstd = sb("rstd", [128, 512], stack=ph)
                    osb = [sb(f"osb{i}", [128, DM], stack=ph) for i in range(2)]
                ot = 0
                for bi_, (s0, n) in enumerate(TBLK):
                    if last and s0 == 0:
                        continue
                    bi = bi_ % 2
                    si = 1 if s0 == 0 else 0
                    S.dma('sp', mixb[bi][:, :, 0:n], mix_d[:, :, s0:s0 + n].rearrange("c p t -> p c t"),
                          r=[f'mix_d{i}' for i in range(8)], w=[f'mixb{bi}'], key=f'mixb{bi}')
                    S.dma('sp', xb[bi][:, :, 0:n], xt_d[:, :, s0:s0 + n], r=['xt_d'], w=[f'xb{bi}'], key=f'xb{bi}')
                    for dm in range(8):
                        b = nextps()
                        for kc in range(8):
                            mm(S, PS[b][:, 0:n], wob[:, kc, dm * 128:(dm + 1) * 128], mixb[bi][:, kc, 0:n], start=(kc == 0), stop=(kc == 7),
                               r=['wob', f'mixb{bi}'], w=[f'ps{b}'])
                        stt(S, xn[bi][:, dm, 0:n], PS[b][:, 0:n], mod[:, l, 16 + dm, si:si + 1], xb[bi][:, dm, 0:n], ALU.mult, ALU.add,
                            r=[f'ps{b}', 'mod', f'xb{bi}'], w=[f'xn{bi}'])
                    if not last:
                        S.dma('pool', xt_d[:, :, s0:s0 + n], xn[bi][:, :, 0:n], r=[f'xn{bi}'], w=['xt_d'], key=f'xn{bi}')
                    else:
                        act(S, sq[:, :, 0:n], xn[bi][:, :, 0:n], AF.Square, r=[f'xn{bi}'], w=['sq'])
                        b = nextps()
                        for kc in range(8):
                            mm(S, PS[b][:, 0:n], C('ones'), sq[:, kc, 0:n], start=(kc == 0), stop=(kc == 7), r=['sq', 'cst'], w=[f'ps{b}'])
                        ts(S, rstd[:, 0:n], PS[b][:, 0:n], 1.0 / DM, EPS, ALU.mult, ALU.add, r=[f'ps{b}'], w=['rstd'])
                        act(S, rstd[:, 0:n], rstd[:, 0:n], AF.Sqrt, r=['rstd'], w=['rstd'])
                        recip(S, rstd[:, 0:n], rstd[:, 0:n], r=['rstd'], w=['rstd'])
                        for dm in range(8):
                            stt(S, xn[bi][:, dm, 0:n], xn[bi][:, dm, 0:n], P('fnw', dm), rstd[:, 0:n], ALU.mult, ALU.mult,
                                r=[f'xn{bi}', 'pv', 'rstd'], w=[f'xn{bi}'])
                        for t in range(n // 128):
                            oi = ot % 2
                            ot += 1
                            for half in range(2):
                                b = nextps()
                                for k in range(4):
                                    dm = half * 4 + k
                                    tr(S, PS[b][:, k * 128:(k + 1) * 128], xn[bi][:, dm, t * 128:(t + 1) * 128], ident,
                                       r=[f'xn{bi}', 'cst'], w=[f'ps{b}'])
                                cp(S, osb[oi][:, half * 512:(half + 1) * 512], PS[b][:, :], r=[f'ps{b}'], w=[f'osb{oi}'],
                                   eng='act' if half else 'dve')
                            r0 = s0 - 256 + t * 128
                            S.dma('pool', out_d[r0:r0 + 128, :], osb[oi][:], r=[f'osb{oi}'], w=['out_d'], key=f'osb{oi}')
                S.flush('pf')
    return nc


def rwkv_phase(nc, S, sb_, PS, nextps, cst, pv, lora, CC, PC, C, P, p_d, mix_d, l, last, dbg):
    with contextlib.ExitStack() as ph:
        def sb(name, shape, dt=F32):
            return sb_(name, shape, dt, stack=ph)

        ident = C('ident')
        raw = sb("raw", [128, NT])
        xwxa = sb("xwxa", [128, NT])
        rT = sb("rT", [128, NT])
        kT = sb("kT", [128, NT])
        vT = sb("vT", [128, NT])
        gT = sb("gT", [128, NT])
        kkT = sb("kkT", [128, NT])
        bonus = sb("bonus", [128, NT])
        YT = sb("YT", [128, NT])
        t5a = sb("t5a", [128, 512])
        t5b = sb("t5b", [128, 512])
        omka = sb("omka", [128, 1])
        mixst = sb("mixst", [128, NT], BF16)
        names = ['lw', 'aa', 'km', 'bb', 'kh', 'bh', 'Kt', 'Bt']
        SB_ = []
        for d_ in range(2):
            bset = dict(
                T_=dict({n_: sb(f"bt{d_}_" + n_, [128, 256]) for n_ in names}, EA=sb(f"EA{d_}", [128, 4, 256])),
                kr=sb(f"kr{d_}", [128, 4, 2, 64]), tot=sb(f"tot{d_}", [128, 4]), pcx=sb(f"pcx{d_}", [128, 4]),
                tokm=sb(f"tokm{d_}", [128, 4, 256]), AB=sb(f"AB{d_}", [128, 4, 256]), Nk=sb(f"Nk{d_}", [128, 2, 256]),
                Q=sb(f"Q{d_}", [128, 256]), Z=sb(f"Z{d_}", [128, 256]), WT=sb(f"WT{d_}", [128, 256]),
                Tst=sb(f"Tst{d_}", [128, 64]), Un=sb(f"Un{d_}", [128, 64]))
            SB_.append(bset)
        YT1 = sb("YT1", [128, NT])
        YTs = [YT, YT1]
        SHARED = {'cst', 'pv', 'lora', 'xwxa', 'rT', 'kT', 'vT', 'kkT', 'omka'}

        def conv(out, onm, j, src=None, snm='raw'):
            src = raw if src is None else src
            base = PC[f'shw{l}']
            w0 = pv[:, base + 0 * 13 + j:base + 0 * 13 + j + 1]
            w1 = pv[:, base + 1 * 13 + j:base + 1 * 13 + j + 1]
            w2 = pv[:, base + 2 * 13 + j:base + 2 * 13 + j + 1]
            ts(S, out[:], src[:], w1, None, ALU.mult, r=[snm, 'pv'], w=[onm])
            for (a0, a1) in ((0, 256), (256, NT)):
                stt(S, out[:, a0 + 1:a1], src[:, a0:a1 - 1], w0, out[:, a0 + 1:a1], ALU.mult, ALU.add, r=[snm, 'pv', onm], w=[onm])
                stt(S, out[:, a0:a1 - 1], src[:, a0 + 1:a1], w2, out[:, a0:a1 - 1], ALU.mult, ALU.add, r=[snm, 'pv', onm], w=[onm])

        S.dma('sp', raw[:], p_d[28, :, :], r=['p_d28'], w=['raw'], key='raw')
        conv(xwxa, 'xwxa', 12)
        act(S, xwxa[0:64, :], xwxa[0:64, :], AF.Tanh, r=['xwxa'], w=['xwxa'])

        if dbg.get('rwkv_stage', 99) <= 1:
            S.flush('pe')
            return
        for p in range(dbg.get('rwkv_pairs', 4)):
            S.dma('sp', raw[:], p_d[16 + p, :, :], r=[f'p_d{16 + p}'], w=['raw'], key='raw')
            S.dma('sp', YT1[:], p_d[20 + p, :, :], r=[f'p_d{20 + p}'], w=['YT_1'], key='yt1raw')
            conv(rT, 'rT', p)
            S.dma('sp', YT[:], p_d[24 + p, :, :], r=[f'p_d{24 + p}'], w=['YT_0'], key='yt0raw')
            conv(kT, 'kT', 4 + p, src=YT1, snm='YT_1')
            conv(vT, 'vT', 8 + p, src=YT, snm='YT_0')
            S.dma('sp', gT[:], p_d[29 + p, :, :], r=[f'p_d{29 + p}'], w=['gT'], key='gT')
            act(S, gT[:], gT[:], AF.Silu, r=['gT'], w=['gT'])
            ts(S, omka[:], P(f'ka{l}', p), -1.0, 1.0, ALU.mult, ALU.add, r=['pv'], w=['omka'])
            ts(S, kkT[:], kT[:], P(f'kk{l}', p), None, ALU.mult, r=['kT', 'pv'], w=['kkT'])
            stt(S, bonus[:], rT[:], P(f'rk{l}', p), kT[:], ALU.mult, ALU.mult, r=['rT', 'kT', 'pv'], w=['bonus'])
            for (s0, n) in TBLK:
                act(S, t5a[:, 0:n], kkT[:, s0:s0 + n], AF.Square, r=['kkT'], w=['t5a'])
                b = nextps()
                mm(S, PS[b][:, 0:n], C('bones'), t5a[:, 0:n], r=['t5a', 'cst'], w=[f'ps{b}'])
                act(S, t5b[:, 0:n], PS[b][:, 0:n], AF.Sqrt, r=[f'ps{b}'], w=['t5b'])
                ts(S, t5b[:, 0:n], t5b[:, 0:n], 1e-12, None, ALU.max, r=['t5b'], w=['t5b'])
                recip(S, t5b[:, 0:n], t5b[:, 0:n], r=['t5b'], w=['t5b'])
                tt(S, kkT[:, s0:s0 + n], kkT[:, s0:s0 + n], t5b[:, 0:n], ALU.mult, r=['kkT', 't5b'], w=['kkT'])
                b = nextps()
                mm(S, PS[b][:, 0:n], C('bones'), bonus[:, s0:s0 + n], r=['bonus', 'cst'], w=[f'ps{b}'])
                tt(S, bonus[:, s0:s0 + n], PS[b][:, 0:n], vT[:, s0:s0 + n], ALU.mult, r=[f'ps{b}', 'vT'], w=['bonus'])
            streams = []
            for d in range(2 if dbg.get('rwkv_stage', 99) > 2 else 0):
                sfx = f"_{d}"
                rec = S.record(lambda k, sfx=sfx: k if (k in SHARED or k.startswith('ps')) else k + sfx)
                bn = [0]

                def nps(d=d, bn=bn):
                    bn[0] = (bn[0] + 1) % 4
                    return 4 * d + bn[0]
                B_ = SB_[d]
                mset(S, YTs[d][:], 0.0, w=['YT'])
                mset(S, B_['Tst'][:], 0.0, w=['Tst'])
                border = list(range(9)) if d == 0 else [0] + list(range(8, 0, -1))
                border = border[:dbg.get('rwkv_blocks', 9)]
                for blk in border:
                    rwkv_block(S, PS, nps, cst, pv, lora, CC, PC, C, P, l, p, d, blk,
                               rT, kT, vT, kkT, xwxa, YTs[d], B_['T_'], B_['kr'], B_['tot'], B_['pcx'], B_['tokm'], B_['AB'],
                               B_['Nk'], B_['Q'], B_['Z'], B_['WT'], B_['Tst'], B_['Un'], omka)
                S.end_record()
                streams.append(rec)
            if dbg.get('rr_merge'):
                S.merge(streams, [0, dbg.get('merge_off', 0)][:len(streams)])
            else:
                S.merge_sched(streams)
            for (s0, n) in TBLK:
                tt(S, YT[:, s0:s0 + n], YT[:, s0:s0 + n], YT1[:, s0:s0 + n], ALU.add, r=['YT_0', 'YT_1'], w=['YT_0'])

            for (s0, n) in TBLK:
                if last and s0 == 0:
                    continue
                b = nextps()
                mm(S, PS[b][:, 0:n], C('bones'), YT[:, s0:s0 + n], r=['YT_0', 'cst'], w=[f'ps{b}'])
                stt(S, t5a[:, 0:n], PS[b][:, 0:n], -1.0 / 64, YT[:, s0:s0 + n], ALU.mult, ALU.add, r=[f'ps{b}', 'YT_0'], w=['t5a'])
                act(S, t5b[:, 0:n], t5a[:, 0:n], AF.Square, r=['t5a'], w=['t5b'])
                b = nextps()
                mm(S, PS[b][:, 0:n], C('bones'), t5b[:, 0:n], r=['t5b', 'cst'], w=[f'ps{b}'])
                ts(S, t5b[:, 0:n], PS[b][:, 0:n], 1.0 / 64, GN_EPS, ALU.mult, ALU.add, r=[f'ps{b}'], w=['t5b'])
                act(S, t5b[:, 0:n], t5b[:, 0:n], AF.Sqrt, r=['t5b'], w=['t5b'])
                recip(S, t5b[:, 0:n], t5b[:, 0:n], r=['t5b'], w=['t5b'])
                tt(S, t5a[:, 0:n], t5a[:, 0:n], t5b[:, 0:n], ALU.mult, r=['t5a', 't5b'], w=['t5a'])
                ts(S, t5a[:, 0:n], t5a[:, 0:n], P(f'lnw{l}', p), P(f'lnb{l}', p), ALU.mult, ALU.add, r=['t5a', 'pv'], w=['t5a'])
                tt(S, t5a[:, 0:n], t5a[:, 0:n], bonus[:, s0:s0 + n], ALU.add, r=['t5a', 'bonus'], w=['t5a'])
                tt(S, mixst[:, s0:s0 + n], t5a[:, 0:n], gT[:, s0:s0 + n], ALU.mult, r=['t5a', 'gT'], w=['mixst'])
            lo = 256 if last else 0
            S.dma('pool', mix_d[4 + p, :, lo:NT], mixst[:, lo:NT], r=['mixst'], w=[f'mix_d{4 + p}'], key='mixst')
            if dbg.get('rwkv_dump') and p == 0:
                for i_, (t_, nm_) in enumerate(((rT, 'rT'), (kT, 'kT'), (vT, 'vT'), (kkT, 'kkT'), (bonus, 'bonus'), (YT, 'YT_0'), (xwxa, 'xwxa'))):
                    S.dma('pool', p_d[i_, :, :], t_[:], r=[nm_], w=[f'p_d{i_}'], key=nm_)
        S.flush('pe')


def rwkv_block(S, PS, nextps, cst, pv, lora, CC, PC, C, P, l, p, d, blk,
               rT, kT, vT, kkT, xwxa, YT, T_, kr, tot, pcx, tokm, AB, Nk, Q, Z, WT, Tst, Un, omka):
    s0 = blk * 256
    bs = slice(s0, s0 + 256)
    lw, aa, km, bb, kh, bh, Kt, Bt = [T_[n_] for n_ in ['lw', 'aa', 'km', 'bb', 'kh', 'bh', 'Kt', 'Bt']]
    EA = T_['EA']

    class _V:
        def __getitem__(self, k):
            return EA[:, 0, :]
    LL = _V()
    ld = l * 2 + d
    b = nextps()
    mm(S, PS[b][:, 0:256], lora[:, ld, p * 128:(p + 1) * 128], xwxa[:, bs], r=['lora', 'xwxa'], w=[f'ps{b}'])
    b2 = nextps()
    mm(S, PS[b2][:, 0:256], lora[:, 4 + ld, p * 128:(p + 1) * 128], xwxa[:, bs], r=['lora', 'xwxa'], w=[f'ps{b2}'])
    act(S, lw[:], PS[b][:, 0:256], AF.Sigmoid, bias=P(f'w0{l}', d * 4 + p), r=[f'ps{b}', 'pv'], w=['lw'])
    ts(S, lw[:], lw[:], -0.6065306597126334, None, ALU.mult, r=['lw'], w=['lw'])
    act(S, aa[:], PS[b2][:, 0:256], AF.Sigmoid, bias=P(f'a0{l}', d * 4 + p), r=[f'ps{b2}', 'pv'], w=['aa'])
    ts(S, km[:], aa[:], P(f'ka{l}', p), omka[:, 0:1], ALU.mult, ALU.add, r=['aa', 'pv', 'omka'], w=['km'])
    tt(S, km[:], km[:], kT[:, bs], ALU.mult, r=['km', 'kT'], w=['km'])
    tt(S, bb[:], kkT[:, bs], aa[:], ALU.mult, r=['kkT', 'aa'], w=['bb'])
    if DEBUG.get('rwkv_stage', 99) <= 3:
        return
    S.op('dve', lambda e: e.tensor_tensor_scan(out=LL[:], data0=C('cmask', 256), data1=lw[:], initial=0.0,
                                               op0=ALU.mult, op1=ALU.add), ['cst', 'lw'], ['EA'], sync_same=True)
    L3 = LL[:].rearrange("p (c t) -> p c t", t=64)
    cp(S, tot[:], LL[:].rearrange("p (c t) -> p c t", t=64)[:, :, 63], r=['EA'], w=['tot'])
    totb = tot[:].rearrange("p (c o) -> p c o", o=1).to_broadcast([128, 4, 64])
    if d == 1:
        tt(S, L3, totb, L3, ALU.subtract, r=['tot', 'EA'], w=['EA'])
        tt(S, LL[:], LL[:], lw[:], ALU.add, r=['EA', 'lw'], w=['EA'])
    krv = kr[:]
    tt(S, EA[:, 1, :], LL[:], lw[:], ALU.subtract, r=['EA', 'lw'], w=['EA'])
    ts(S, EA[:, 2, :], LL[:], -1.0, None, ALU.mult, r=['EA'], w=['EA'])
    tt(S, EA[:, 3, :].rearrange("p (c t) -> p c t", t=64), totb, L3, ALU.subtract, r=['tot', 'EA'], w=['EA'])
    act(S, EA[:].rearrange("p a x -> p (a x)"), EA[:].rearrange("p a x -> p (a x)"), AF.Exp, r=['EA'], w=['EA'])
    act(S, pcx[:], tot[:], AF.Exp, r=['tot'], w=['pcx'])
    v3 = lambda ap: ap.rearrange("p (c t) -> p c t", t=64)
    tt(S, krv[:, :, 1, :], v3(rT[:, bs]), v3(EA[:, 0, :]), ALU.mult, r=['rT', 'EA'], w=['kr'])
    tt(S, krv[:, :, 0, :], v3(kkT[:, bs]), v3(EA[:, 1, :]), ALU.mult, r=['kkT', 'EA'], w=['kr'])
    tt(S, kh[:], km[:], EA[:, 2, :], ALU.mult, r=['km', 'EA'], w=['kh'])
    tt(S, bh[:], bb[:], EA[:, 2, :], ALU.mult, r=['bb', 'EA'], w=['bh'])
    tt(S, Kt[:], km[:], EA[:, 3, :], ALU.mult, r=['km', 'EA'], w=['Kt'])
    tt(S, Bt[:], bb[:], EA[:, 3, :], ALU.mult, r=['bb', 'EA'], w=['Bt'])
    if DEBUG.get('rwkv_stage', 99) <= 4:
        return
    srcs = [(None, 'kr'), (Kt[:], 'Kt'), (Bt[:], 'Bt'), (vT[:, bs], 'vT')]
    for half in range(2):
        b = nextps()
        for kk_ in range(2):
            kind = half * 2 + kk_
            src, snm = srcs[kind]
            for c in range(4):
                for h in range(2):
                    lh = krv[hsl(h), c, 0, :] if src is None else src[hsl(h), c * 64:(c + 1) * 64]
                    mm(S, PS[b][hsl(h), kk_ * 256 + c * 64:kk_ * 256 + (c + 1) * 64], lh,
                       cst[hsl(h), CC['ident'] + 64 * h:CC['ident'] + 64 * h + 64], tp=(64 * h, 64 * h),
                       r=[snm, 'cst'], w=[f'ps{b}'])
        cp(S, tokm[:, half * 2:half * 2 + 2, :], PS[b][:, :].rearrange("p (k x) -> p k x", x=256), r=[f'ps{b}'], w=['tokm'],
           eng='act' if half else 'dve')
    if DEBUG.get('rwkv_stage', 99) <= 5:
        return
    bA = [nextps(), nextps()]
    for c in range(4):
        pb = bA[c // 2]
        o0 = (c % 2) * 256
        for h in range(2):
            rhs = krv[hsl(h), c, :, :].rearrange("p a t -> p (a t)")
            mm(S, PS[pb][hsl(h), o0:o0 + 128], kh[hsl(h), c * 64:(c + 1) * 64], rhs, tp=(64 * h, 64 * h), r=['kh', 'kr'], w=[f'ps{pb}'])
            mm(S, PS[pb][hsl(h), o0 + 128:o0 + 256], bh[hsl(h), c * 64:(c + 1) * 64], rhs, tp=(64 * h, 64 * h), r=['bh', 'kr'], w=[f'ps{pb}'])
    mab = cst[:, CC[f'mab{d}']:CC[f'mab{d}'] + 256]
    for i2 in range(2):
        tt(S, AB[:, 2 * i2:2 * i2 + 2, :], PS[bA[i2]][:, :].rearrange("p (c x) -> p c x", x=256),
           mab.rearrange("p (o x) -> p o x", o=1).to_broadcast([128, 2, 256]), ALU.mult, r=[f'ps{bA[i2]}', 'cst'], w=['AB'])
    b = nextps()
    for c in range(4):
        for h in range(2):
            mm(S, PS[b][hsl(h), c * 64:(c + 1) * 64], krv[hsl(h), c, 0, :], bh[hsl(h), c * 64:(c + 1) * 64], tp=(64 * h, 64 * h),
               r=['kr', 'bh'], w=[f'ps{b}'])
    tt(S, Nk[:, 1, :], PS[b][:, 0:256], cst[:, CC[f'mn{d}']:CC[f'mn{d}'] + 256], ALU.mult, r=[f'ps{b}', 'cst'], w=['Nk'])
    if DEBUG.get('rwkv_stage', 99) <= 6:
        return
    AbT = AB[:, :, 128:192]
    cp(S, Nk[:, 0, :].rearrange("p (c t) -> p c t", t=64), AbT, r=['AB'], w=['Nk'], eng='act')
    tt(S, Q[:].rearrange("p (c t) -> p c t", t=64), C('identrep', 256).rearrange("p (c t) -> p c t", t=64), AbT, ALU.subtract,
       r=['cst', 'AB'], w=['Q'])
    for it in range(5):
        b = nextps()
        for c in range(4):
            cs_ = slice(c * 64, (c + 1) * 64)
            for h in range(2):
                if it < 4:
                    mm(S, PS[b][hsl(h), c * 64:(c + 1) * 64], Nk[hsl(h), 1, cs_], Nk[hsl(h), 0, cs_], tp=(64 * h, 64 * h),
                       r=['Nk'], w=[f'ps{b}'])
                mm(S, PS[b][hsl(h), 256 + c * 64:256 + (c + 1) * 64], Nk[hsl(h), 0, cs_], Nk[hsl(h), 1, cs_], tp=(64 * h, 64 * h),
                   r=['Nk'], w=[f'ps{b}'])
        if it < 4:
            cp(S, Nk[:].rearrange("p a x -> p (a x)"), PS[b][:, :], r=[f'ps{b}'], w=['Nk'])
        else:
            cp(S, Nk[:, 1, :], PS[b][:, 256:512], r=[f'ps{b}'], w=['Nk'])
        b2 = nextps()
        for c in range(4):
            cs_ = slice(c * 64, (c + 1) * 64)
            for h in range(2):
                mm(S, PS[b2][hsl(h), cs_], Nk[hsl(h), 1, cs_], Q[hsl(h), cs_], tp=(64 * h, 64 * h), r=['Nk', 'Q'], w=[f'ps{b2}'])
        tt(S, Q[:], Q[:], PS[b2][:, 0:256], ALU.add, r=['Q', f'ps{b2}'], w=['Q'])
    if DEBUG.get('rwkv_stage', 99) <= 7:
        return
    b = nextps()
    for c in range(4):
        cs_ = slice(c * 64, (c + 1) * 64)
        for h in range(2):
            mm(S, PS[b][hsl(h), cs_], AB[hsl(h), c, 0:64], tokm[hsl(h), 3, cs_], tp=(64 * h, 64 * h), r=['AB', 'tokm'], w=[f'ps{b}'])
    cp(S, Z[:], PS[b][:, 0:256], r=[f'ps{b}'], w=['Z'], eng='act')
    b = nextps()
    for c in range(4):
        cs_ = slice(c * 64, (c + 1) * 64)
        for h in range(2):
            mm(S, PS[b][hsl(h), cs_], tokm[hsl(h), 0, cs_], Q[hsl(h), cs_], tp=(64 * h, 64 * h), r=['tokm', 'Q'], w=[f'ps{b}'])
    cp(S, WT[:], PS[b][:, 0:256], r=[f'ps{b}'], w=['WT'])
    if DEBUG.get('rwkv_stage', 99) <= 8:
        return
    corder = range(4) if d == 0 else range(3, -1, -1)
    for c in corder:
        cs_ = slice(c * 64, (c + 1) * 64)
        b = nextps()
        for h in range(2):
            mm(S, PS[b][hsl(h), 0:64], Q[hsl(h), cs_], Z[hsl(h), cs_], start=True, stop=False, tp=(64 * h, 64 * h), r=['Q', 'Z'], w=[f'ps{b}'])
            mm(S, PS[b][hsl(h), 0:64], WT[hsl(h), cs_], Tst[hsl(h), :], start=False, stop=True, tp=(64 * h, 64 * h),
               r=['WT', 'Tst'], w=[f'ps{b}'])
        ts(S, Un[:], PS[b][:, 0:64], -1.0, None, ALU.mult, r=[f'ps{b}'], w=['Un'])
        b2 = nextps()
        for h in range(2):
            mm(S, PS[b2][hsl(h), 0:64], Tst[hsl(h), :], kr[hsl(h), c, 1, :], start=True, stop=False, tp=(64 * h, 64 * h),
               r=['Tst', 'kr'], w=[f'ps{b2}'])
            mm(S, PS[b2][hsl(h), 0:64], tokm[hsl(h), 3, cs_], AB[hsl(h), c, 64:128], start=False, stop=False, tp=(64 * h, 64 * h),
               r=['tokm', 'AB'], w=[f'ps{b2}'])
            mm(S, PS[b2][hsl(h), 0:64], Un[hsl(h), :], AB[hsl(h), c, 192:256], start=False, stop=True, tp=(64 * h, 64 * h),
               r=['Un', 'AB'], w=[f'ps{b2}'])
        b3 = nextps()
        for h in range(2):
            mm(S, PS[b3][hsl(h), 0:64], tokm[hsl(h), 1, cs_], tokm[hsl(h), 3, cs_], start=True, stop=False, tp=(64 * h, 64 * h),
               r=['tokm'], w=[f'ps{b3}'])
            mm(S, PS[b3][hsl(h), 0:64], tokm[hsl(h), 2, cs_], Un[hsl(h), :], start=False, stop=True, tp=(64 * h, 64 * h),
               r=['tokm', 'Un'], w=[f'ps{b3}'])
        stt(S, Tst[:], Tst[:], pcx[:, c:c + 1], PS[b3][:, 0:64], ALU.mult, ALU.add, r=['Tst', 'pcx', f'ps{b3}'], w=['Tst'])
        ys = slice(s0 + c * 64, s0 + (c + 1) * 64)
        tt(S, YT[:, ys], YT[:, ys], PS[b2][:, 0:64], ALU.add, r=['YT', f'ps{b2}'], w=['YT'])


_CACHE = {}


def kernel(x, c, ctx, c_ctx, norm_w, w_mod, b_mod, w_in, ret_log_gamma, ret_norm_w,
           rwkv_shift_w, rwkv_w0, rwkv_w2, rwkv_a0, rwkv_a2, rwkv_k_k, rwkv_k_a, rwkv_r_k,
           rwkv_ln_w, rwkv_ln_b, w_out, final_norm_w):
    inp = dict(x=x, c=c, ctx=ctx, c_ctx=c_ctx, norm_w=norm_w, w_mod=w_mod, b_mod=b_mod, w_in=w_in,
               ret_log_gamma=ret_log_gamma, ret_norm_w=ret_norm_w, rwkv_shift_w=rwkv_shift_w, rwkv_w0=rwkv_w0,
               rwkv_w2=rwkv_w2, rwkv_a0=rwkv_a0, rwkv_a2=rwkv_a2, rwkv_k_k=rwkv_k_k, rwkv_k_a=rwkv_k_a,
               rwkv_r_k=rwkv_r_k, rwkv_ln_w=rwkv_ln_w, rwkv_ln_b=rwkv_ln_b, w_out=w_out, final_norm_w=final_norm_w)
    inp = {k: np.asarray(v, dtype=np.float32) for k, v in inp.items()}
    cst, CC, rope = _consts()
    pvs = [_pvec(inp, b) for b in range(8)]
    PC = pvs[0][1]
    zz = np.zeros_like(inp['rwkv_w2'])
    lw_ = np.concatenate([inp['rwkv_w2'], zz], axis=2).transpose(2, 0, 1, 3).reshape(128, 4, 512)
    la_ = np.concatenate([zz, inp['rwkv_a2']], axis=2).transpose(2, 0, 1, 3).reshape(128, 4, 512)
    lora = np.ascontiguousarray(np.concatenate([lw_, la_], axis=1))
    nc = build(CC, PC, cst.shape[1], pvs[0][0].shape[1], DEBUG)
    in_maps = []
    for b in range(8):
        in_maps.append({
            "x": np.ascontiguousarray(inp['x'][b]), "ctx": np.ascontiguousarray(inp['ctx'][b]),
            "cst": cst, "pv": pvs[b][0], "lora": lora, "rope": rope,
            "w_mod": inp['w_mod'], "w_in": inp['w_in'], "w_out": inp['w_out'],
        })
    res = run_bass_kernel_spmd(nc, in_maps, core_ids=list(range(8)))
    _CACHE['res'] = res
    return np.stack([np.asarray(r["out"], dtype=np.float32) for r in res.results], axis=0)
```

```python
import contextlib
import numpy as np
import concourse.bass as bass
import concourse.mybir as mybir
from concourse.bass_utils import run_bass_kernel_spmd

F32 = mybir.dt.float32
BF16 = mybir.dt.bfloat16
AF = mybir.ActivationFunctionType
ALU = mybir.AluOpType

NT, NC_, NL, DM = 2304, 256, 2048, 1024
TBLK = [(0, 256), (256, 512), (768, 512), (1280, 512), (1792, 512)]
DEPTH = 2
EPS = 1e-6
GN_EPS = 64e-5
ENGF = {'pe': 'tensor', 'act': 'scalar', 'dve': 'vector', 'pool': 'gpsimd', 'sp': 'sync'}

DEBUG = {}


class Sched:
    def __init__(self, nc, stack, ndma=40):
        self.nc = nc
        self.nblk = 0
        self.esem = {e: stack.enter_context(nc.semaphore(f"sem_e_{e}")) for e in ENGF}
        self.ecount = {e: 0 for e in ENGF}
        self.dpool = [stack.enter_context(nc.semaphore(f"sem_d_{i}")) for i in range(ndma)]
        self.dtotal = [0] * ndma
        self.rec = None
        self.keymap = None
        self.reset()

    def reset(self):
        self.ops = []
        self.last_w = {}
        self.readers = {}
        self.eseq = {}

    def record(self, keymap):
        self.rec = []
        self.keymap = keymap
        return self.rec

    def end_record(self):
        self.rec = None
        self.keymap = None

    def merge_sched(self, streams):
        its = [list(x) for x in streams]
        ptr = [0] * len(its)
        E, Wt, Rt = {}, {}, {}
        LAT = DEBUG.get('lat', 16.0)

        def start_of(o):
            eng, fn, r, w, dk, ss, cost = o
            t = E.get(eng, 0.0)
            for k in list(r) + list(w):
                if k in Wt:
                    t = max(t, Wt[k][0] + (LAT if Wt[k][1] != eng else 0.0))
            for k in w:
                if k in Rt:
                    t = max(t, Rt[k])
            return t
        while True:
            best, bt = None, None
            for i, x in enumerate(its):
                if ptr[i] < len(x):
                    t = start_of(x[ptr[i]])
                    if bt is None or t < bt:
                        best, bt = i, t
            if best is None:
                break
            o = its[best][ptr[best]]
            ptr[best] += 1
            eng, fn, r, w, dk, ss, cost = o
            end = bt + cost
            E[eng] = end
            for k in w:
                Wt[k] = (end, eng)
                Rt.pop(k, None)
            for k in r:
                Rt[k] = max(Rt.get(k, 0.0), end)
            self._add(*o)

    def merge(self, streams, offsets=None):
        its = [list(x) for x in streams]
        offsets = offsets or [0] * len(its)
        n = max((len(x) + o for x, o in zip(its, offsets)), default=0)
        for i in range(n):
            for x, o in zip(its, offsets):
                if 0 <= i - o < len(x):
                    self._add(*x[i - o])

    def _add(self, eng, fn, r, w, dma_key=None, sync_same=False, cost=0.3):
        if getattr(self, 'rec', None) is not None:
            km = self.keymap
            self.rec.append((eng, fn, [km(k) for k in r], [km(k) for k in w], dma_key, sync_same, cost))
            return
        idx = len(self.ops)
        deps = set()
        raw = set()
        for k in r:
            if k in self.last_w:
                deps.add(self.last_w[k])
                raw.add(self.last_w[k])
        for k in w:
            if k in self.last_w:
                deps.add(self.last_w[k])
            for x in self.readers.get(k, {}).values():
                deps.update(x)
        for k in w:
            self.last_w[k] = idx
            self.readers[k] = {}
        for k in r:
            rd = self.readers.setdefault(k, {})
            if dma_key is None:
                rd[eng] = [idx]
            else:
                rd.setdefault('dma', []).append(idx)
        deps.discard(idx)
        self.eseq[eng] = self.eseq.get(eng, 0) + 1
        self.ops.append(dict(eng=eng, fn=fn, deps=deps, dma=dma_key, need=False, cnt=0, ss=sync_same, raw=raw,
                             eseq=self.eseq[eng]))

    @staticmethod
    def _same_sync(a, o, d):
        if a['ss']:
            return True
        return a['eng'] != 'pe' and d in o['raw'] and (o['eseq'] - a['eseq']) <= 4

    def op(self, eng, fn, r=(), w=(), sync_same=False, cost=0.3):
        self._add(eng, fn, r, w, sync_same=sync_same, cost=cost)

    def dma(self, q, out, in_, r=(), w=(), key=None):
        self._add(q, lambda e: e.dma_start(out=out, in_=in_), r, w, dma_key=key)

    def flush(self, name=None):
        ops = self.ops
        if not ops:
            return
        nc = self.nc
        self.nblk += 1
        name = f"{name or 'blk'}{self.nblk}"
        for o in ops:
            for d in o['deps']:
                a = ops[d]
                if a['dma'] is not None:
                    continue
                if a['eng'] == o['eng'] and o['dma'] is None and not self._same_sync(a, o, d):
                    continue
                a['need'] = True
        cnt, dcnt, didx = dict(self.ecount), {}, {}
        for o in ops:
            if o['dma'] is not None:
                k = o['dma']
                if k not in didx:
                    didx[k] = len(didx)
                    assert didx[k] < len(self.dpool), "too many DMA keys in one block"
                    dcnt[k] = self.dtotal[didx[k]]
                dcnt[k] += 16
                o['cnt'] = dcnt[k]
            elif o['need']:
                cnt[o['eng']] += 1
                o['cnt'] = cnt[o['eng']]
        for k, i in didx.items():
            self.dtotal[i] = dcnt[k]
        self.ecount = cnt
        engs = sorted(set(o['eng'] for o in ops))
        with contextlib.ExitStack() as st:
            esem = self.esem
            dsem = {k: self.dpool[i] for k, i in didx.items()}
            block = st.enter_context(nc.Block(name))

            def emit(e_name):
                def body(e):
                    waited = {}
                    my_dma = {}
                    for o in ops:
                        if o['eng'] != e_name:
                            continue
                        need = {}
                        for d in o['deps']:
                            a = ops[d]
                            if a['dma'] is not None:
                                key = ('d', a['dma'])
                            elif a['eng'] == e_name and o['dma'] is None and not self._same_sync(a, o, d):
                                continue
                            else:
                                key = ('e', a['eng'])
                            if a['cnt'] > need.get(key, 0):
                                need[key] = a['cnt']
                        for key, val in need.items():
                            if waited.get(key, 0) >= val:
                                continue
                            waited[key] = val
                            sem = dsem[key[1]] if key[0] == 'd' else esem[key[1]]
                            e.wait_ge(sem, val)
                        ins = o['fn'](e)
                        if o['dma'] is not None:
                            ins.then_inc(dsem[o['dma']], 16)
                            my_dma[o['dma']] = o['cnt']
                        elif o['need']:
                            ins.then_inc(esem[e_name], 1)
                    for k, v in my_dma.items():
                        if waited.get(('d', k), 0) < v:
                            e.wait_ge(dsem[k], v)
                return body

            for e_name in engs:
                getattr(block, ENGF[e_name])(emit(e_name))
        self.reset()


def _fs(ap):
    try:
        return float(ap.free_size())
    except Exception:
        return 256.0


def _ec(ap):
    return 0.07 + _fs(ap) / 960.0


def mm(S, out, lhsT, rhs, start=True, stop=True, tp=None, r=(), w=()):
    kw = {} if tp is None else {'tile_position': tp}
    S.op('pe', lambda e: e.matmul(out, lhsT, rhs, start=start, stop=stop, **kw), r, w, cost=_fs(rhs) * 4 / 2400.0 + 0.012)


def tr(S, out, in_, ident, r=(), w=()):
    S.op('pe', lambda e: e.transpose(out, in_, ident), r, w)


def act(S, out, in_, func, r=(), w=(), scale=None, bias=None):
    kw = {}
    if scale is not None:
        kw['scale'] = scale
    if bias is not None:
        kw['bias'] = bias
    S.op('act', lambda e: e.activation(out=out, in_=in_, func=func, **kw), r, w, cost=_ec(out) + 0.15)


def tt(S, out, in0, in1, op, r=(), w=(), eng='dve'):
    S.op(eng, lambda e: e.tensor_tensor(out=out, in0=in0, in1=in1, op=op), r, w, cost=_ec(out))


def ts(S, out, in0, s1, s2, op0, op1=None, r=(), w=(), eng='dve'):
    if op1 is None:
        S.op(eng, lambda e: e.tensor_scalar(out=out, in0=in0, scalar1=s1, scalar2=None, op0=op0), r, w, cost=_ec(out))
    else:
        S.op(eng, lambda e: e.tensor_scalar(out=out, in0=in0, scalar1=s1, scalar2=s2, op0=op0, op1=op1), r, w, cost=_ec(out))


def stt(S, out, in0, scalar, in1, op0, op1, r=(), w=(), eng='dve'):
    S.op(eng, lambda e: e.scalar_tensor_tensor(out=out, in0=in0, scalar=scalar, in1=in1, op0=op0, op1=op1), r, w, cost=_ec(out))


def cp(S, out, in_, r=(), w=(), eng='dve'):
    if eng == 'act':
        S.op('act', lambda e: e.activation(out=out, in_=in_, func=AF.Copy), r, w, cost=_ec(out) + 0.15)
    else:
        S.op(eng, lambda e: e.tensor_copy(out=out, in_=in_), r, w, cost=_ec(out))


def recip(S, out, in_, r=(), w=()):
    S.op('dve', lambda e: e.reciprocal(out=out, in_=in_), r, w)


def mset(S, ap, val, r=(), w=(), eng='dve'):
    S.op(eng, lambda e: e.memset(ap, val), r, w)


def hsl(h):
    return slice(64 * h, 64 * h + 64)


def _consts():
    cols = {}
    parts = []
    off = [0]

    def add(name, a):
        a = np.ascontiguousarray(a, dtype=np.float32).reshape(128, -1)
        cols[name] = off[0]
        off[0] += a.shape[1]
        parts.append(a)

    p = np.arange(128)
    f = np.arange(128)
    add('ident', np.eye(128))
    d = p % 64
    partner = np.where((d % 32) < 16, p + 16, p - 16)
    perm = np.zeros((128, 128))
    perm[partner, p] = 1.0
    add('perm', perm)
    bones = np.zeros((128, 128))
    bones[:64, :64] = 1.0
    bones[64:, 64:] = 1.0
    add('bones', bones)
    add('ones', np.ones((128, 128)))
    add('jcol', p.astype(np.float64)[:, None])
    add('jrev', (127 - p).astype(np.float64)[:, None])
    add('irow1', np.broadcast_to((f + 1.0)[None, :], (128, 128)))
    add('irowb', np.broadcast_to((128.0 - f)[None, :], (128, 128)))
    rel = f[None, :] - p[:, None]
    add('relpos', np.maximum(rel, 0))
    add('relneg', np.maximum(-rel, 0))
    add('triu', (rel >= 0))
    add('tril', (rel <= 0))
    pi = (p % 64)[:, None]
    ph = np.arange(64)[None, :]
    su, iu, sl, il = (pi < ph), (pi <= ph), (pi > ph), (pi >= ph)
    add('mab0', np.concatenate([su, iu, su, iu], axis=1))
    add('mab1', np.concatenate([sl, il, sl, il], axis=1))
    add('mn0', np.tile(sl, (1, 4)))
    add('mn1', np.tile(su, (1, 4)))
    add('identrep', np.tile(np.eye(64), (2, 4)))
    cm = np.ones((128, 256))
    cm[:, ::64] = 0.0
    add('cmask', cm)
    t = np.arange(NL)
    rowp = (t // 64).astype(np.float64)
    colp = (t % 64).astype(np.float64)
    inv = 10000.0 ** (-np.arange(16, dtype=np.float64) / 16.0)
    pos = np.where((d < 32)[:, None], rowp[None, :], colp[None, :])
    ang = (pos.astype(np.float32) * inv.astype(np.float32)[d % 16][:, None]).astype(np.float32)
    sgn = np.where((d % 32) < 16, -1.0, 1.0)[:, None]
    rope = np.ascontiguousarray(np.stack([np.cos(ang), np.sin(ang) * sgn], axis=1).astype(np.float32))
    return np.concatenate(parts, axis=1), cols, rope


def _pvec(inp, b):
    cols = {}
    parts = []
    off = [0]

    def add(name, a):
        a = np.ascontiguousarray(a, dtype=np.float32).reshape(128, -1)
        cols[name] = off[0]
        off[0] += a.shape[1]
        parts.append(a)

    add('c', inp['c'][b].reshape(8, 128).T)
    add('cctx', inp['c_ctx'].reshape(8, 128).T)
    add('fnw', inp['final_norm_w'].reshape(8, 128).T)
    for l in range(DEPTH):
        add(f'normw{l}', inp['norm_w'][l].reshape(8, 128).T)
        add(f'bmod{l}', inp['b_mod'][l].reshape(24, 128).T)
        add(f'retnw{l}', inp['ret_norm_w'][l].reshape(4, 128).T)
        add(f'shw{l}', inp['rwkv_shift_w'][l].reshape(3, 13, 128).transpose(2, 0, 1))
        add(f'w0{l}', inp['rwkv_w0'][l].reshape(2, 4, 128).transpose(2, 0, 1))
        add(f'a0{l}', inp['rwkv_a0'][l].reshape(2, 4, 128).transpose(2, 0, 1))
        add(f'kk{l}', inp['rwkv_k_k'][l].reshape(4, 128).T)
        add(f'ka{l}', inp['rwkv_k_a'][l].reshape(4, 128).T)
        add(f'rk{l}', inp['rwkv_r_k'][l].reshape(4, 128).T)
        add(f'lnw{l}', inp['rwkv_ln_w'][l].reshape(4, 128).T)
        add(f'lnb{l}', inp['rwkv_ln_b'][l].reshape(4, 128).T)
        lg = inp['ret_log_gamma'][l]
        add(f'lgcol{l}', np.repeat(lg.reshape(2, 4, 2), 64, axis=2).transpose(2, 0, 1))
        add(f'lgrep{l}', np.broadcast_to(lg.reshape(1, 16), (128, 16)))
    return np.concatenate(parts, axis=1), cols


def build(CC, PC, ncst, npv, dbg=None):
    dbg = dbg or {}
    nlayers = dbg.get('layers', DEPTH)
    nc = bass.Bass("TRN2", target_bir_lowering=False)
    x_d = nc.dram_tensor("x", [NL, DM], F32, kind="ExternalInput").ap()
    ctx_d = nc.dram_tensor("ctx", [NC_, DM], F32, kind="ExternalInput").ap()
    cst_d = nc.dram_tensor("cst", [128, ncst], F32, kind="ExternalInput").ap()
    pv_d = nc.dram_tensor("pv", [128, npv], F32, kind="ExternalInput").ap()
    rope_d = nc.dram_tensor("rope", [128, 2, NL], F32, kind="ExternalInput").ap()
    lora_d = nc.dram_tensor("lora", [128, 8, 512], F32, kind="ExternalInput").ap()
    wmod_d = nc.dram_tensor("w_mod", [DEPTH, DM, 3 * DM], F32, kind="ExternalInput").ap()
    win_d = nc.dram_tensor("w_in", [DEPTH, DM, 4224], F32, kind="ExternalInput").ap()
    wout_d = nc.dram_tensor("w_out", [DEPTH, DM, DM], F32, kind="ExternalInput").ap()
    out_d = nc.dram_tensor("out", [NL, DM], F32, kind="ExternalOutput").ap()
    dk = "ExternalOutput" if dbg.get('dump') else "Internal"
    xt_d = nc.dram_tensor("xt_scr", [128, 8, NT], F32, kind=dk).ap()
    p_d = nc.dram_tensor("p_scr", [33, 128, NT], F32, kind=dk).ap()
    mix_d = nc.dram_tensor("mix_scr", [8, 128, NT], BF16, kind=dk).ap()

    es = contextlib.ExitStack()
    with es:
        S = Sched(nc, es)
        DEBUG['_p_d'] = p_d
        uid = [0]

        def sb(name, shape, dt=F32, stack=es):
            uid[0] += 1
            return stack.enter_context(nc.sbuf_tensor(f"{name}_{uid[0]}", shape, dt))

        cst = sb("cst_sb", [128, ncst])
        pv = sb("pv_sb", [128, npv])
        lora = sb("lora_sb", [128, 8, 512])
        mod = sb("mod_sb", [128, DEPTH, 24, 2])
        Amod = sb("amod_sb", [128, DEPTH, 8, 2])
        PS = [es.enter_context(nc.psum_tensor(f"ps{i}", [128, 512], F32)) for i in range(8)]

        def C(name, n=128, rows=slice(0, 128)):
            return cst[rows, CC[name]:CC[name] + n]

        def P(name, j=0, rows=slice(0, 128)):
            return pv[rows, PC[name] + j:PC[name] + j + 1]

        ident = C('ident')
        psn = [0]

        def nextps():
            psn[0] = (psn[0] + 1) % 8
            return psn[0]

        S.dma('sp', cst[:], cst_d[:, :], w=['cst'], key='cst')
        S.dma('sp', pv[:], pv_d[:, :], w=['pv'], key='pv')
        S.dma('sp', lora[:], lora_d[:, :, :], w=['lora'], key='lora')
        ph0 = contextlib.ExitStack()
        if True:
            ph = ph0
            xin = [sb(f"xin{i}", [128, DM], stack=ph) for i in range(2)]
            xo = [sb(f"xo{i}", [128, 8, 128], stack=ph) for i in range(2)]
            PA_BUFS = (sb("cs", [128, 8, 2], stack=ph0), [sb(f"wm{i}", [128, 8, 128], stack=ph0) for i in range(2)])
            for i in range(18):
                bi = i % 2
                src = ctx_d[i * 128:(i + 1) * 128, :] if i < 2 else x_d[(i - 2) * 128:(i - 1) * 128, :]
                S.dma('sp', xin[bi][:], src, w=[f'xin{bi}'], key=f'xin{bi}')
                for half in range(2):
                    b = nextps()
                    for k in range(4):
                        kc = half * 4 + k
                        tr(S, PS[b][:, k * 128:(k + 1) * 128], xin[bi][:, kc * 128:(kc + 1) * 128], ident,
                           r=[f'xin{bi}', 'cst'], w=[f'ps{b}'])
                    cp(S, xo[bi][:, half * 4:half * 4 + 4, :],
                       PS[b][:, :].rearrange("p (k t) -> p k t", t=128),
                       r=[f'ps{b}'], w=[f'xo{bi}'], eng='act' if half else 'dve')
                S.dma('pool', xt_d[:, :, i * 128:(i + 1) * 128], xo[bi][:], r=[f'xo{bi}'], w=['xt_d'], key=f'xo{bi}')

        with contextlib.ExitStack() as ph:
            cs, wm = PA_BUFS
            act(S, cs[:, :, 0], pv[:, PC['c']:PC['c'] + 8], AF.Silu, r=['pv'], w=['cs'])
            act(S, cs[:, :, 1], pv[:, PC['cctx']:PC['cctx'] + 8], AF.Silu, r=['pv'], w=['cs'])
            it = 0
            for l in range(nlayers):
                for oc in range(24):
                    bi = it % 2
                    it += 1
                    S.dma('sp', wm[bi][:], wmod_d[l, :, oc * 128:(oc + 1) * 128].rearrange("(kc p) n -> p kc n", p=128),
                          w=[f'wm{bi}'], key=f'wm{bi}')
                    b = nextps()
                    for kc in range(8):
                        mm(S, PS[b][:, 0:2], wm[bi][:, kc, :], cs[:, kc, :], start=(kc == 0), stop=(kc == 7),
                           r=[f'wm{bi}', 'cs'], w=[f'ps{b}'])
                    ts(S, mod[:, l, oc, :], PS[b][:, 0:2], P(f'bmod{l}', oc), None, ALU.add,
                       r=[f'ps{b}', 'pv'], w=['mod'])
                for kc in range(8):
                    ts(S, Amod[:, l, kc, :], mod[:, l, 8 + kc, :], 1.0, P(f'normw{l}', kc), ALU.add, ALU.mult,
                       r=['mod', 'pv'], w=['amod'])
            S.flush('pa')
        ph0.close()

        for l in range(nlayers):
            last = (l == DEPTH - 1)
            lastm = last and not dbg.get('nolast')
            with contextlib.ExitStack() as ph:
                hT = sb("hT", [128, 8, NT], BF16, stack=ph)
                with contextlib.ExitStack() as ph2:
                    xb = [sb(f"xb{i}", [128, 8, 512], stack=ph2) for i in range(2)]
                    sq = sb("sq", [128, 8, 512], stack=ph2)
                    rstd = sb("rstd", [128, 512], stack=ph2)
                    tmpf = [sb(f"tmpf{i}", [128, 512], stack=ph2) for i in range(2)]
                    for bi_, (s0, n) in enumerate(TBLK):
                        bi = bi_ % 2
                        si = 1 if s0 == 0 else 0
                        S.dma('sp', xb[bi][:, :, 0:n], xt_d[:, :, s0:s0 + n], r=['xt_d'], w=[f'xb{bi}'], key=f'xb{bi}')
                        act(S, sq[:, :, 0:n], xb[bi][:, :, 0:n], AF.Square, r=[f'xb{bi}'], w=['sq'])
                        b = nextps()
                        for kc in range(8):
                            mm(S, PS[b][:, 0:n], C('ones'), sq[:, kc, 0:n], start=(kc == 0), stop=(kc == 7),
                               r=['sq', 'cst'], w=[f'ps{b}'])
                        ts(S, rstd[:, 0:n], PS[b][:, 0:n], 1.0 / DM, EPS, ALU.mult, ALU.add, r=[f'ps{b}'], w=['rstd'])
                        act(S, rstd[:, 0:n], rstd[:, 0:n], AF.Sqrt, r=['rstd'], w=['rstd'])
                        recip(S, rstd[:, 0:n], rstd[:, 0:n], r=['rstd'], w=['rstd'])
                        for kc in range(8):
                            tb = kc % 2
                            stt(S, tmpf[tb][:, 0:n], xb[bi][:, kc, 0:n], Amod[:, l, kc, si:si + 1], rstd[:, 0:n],
                                ALU.mult, ALU.mult, r=[f'xb{bi}', 'amod', 'rstd'], w=[f'tmpf{tb}'])
                            act(S, hT[:, kc, s0:s0 + n], tmpf[tb][:, 0:n], AF.Identity, bias=mod[:, l, kc, si:si + 1],
                                r=[f'tmpf{tb}', 'mod'], w=['hT'])
                    wst = [sb(f"wst{i}", [128, 8, 128], stack=ph2) for i in range(2)]
                    wbf = [sb(f"wbf{i}", [128, 8, 128], BF16, stack=ph2) for i in range(2)]
                    pst = [sb(f"pst{i}", [128, NT], stack=ph2) for i in range(2)]
                    ev = 0
                    for cc in range(33):
                        bi = cc % 2
                        S.dma('sp', wst[bi][:], win_d[l, :, cc * 128:(cc + 1) * 128].rearrange("(kc p) n -> p kc n", p=128),
                              w=[f'wst{bi}'], key=f'wst{bi}')
                        cp(S, wbf[bi][:], wst[bi][:], r=[f'wst{bi}'], w=[f'wbf{bi}'], eng='dve' if cc % 2 else 'act')
                        for (s0, n) in TBLK:
                            b = nextps()
                            for kc in range(8):
                                mm(S, PS[b][:, 0:n], wbf[bi][:, kc, :], hT[:, kc, s0:s0 + n], start=(kc == 0), stop=(kc == 7),
                                   r=[f'wbf{bi}', 'hT'], w=[f'ps{b}'])
                            ev += 1
                            cp(S, pst[bi][:, s0:s0 + n], PS[b][:, 0:n], r=[f'ps{b}'], w=[f'pst{bi}'],
                               eng='act' if ev % 2 else 'dve')
                        S.dma('pool', p_d[cc, :, :], pst[bi][:], r=[f'pst{bi}'], w=[f'p_d{cc}'], key=f'pst{bi}')
                    S.flush('pc')
            if dbg.get('stop') == 'pc' or (l == 1 and dbg.get('l1stop') == 'pc'):
                break

            if not dbg.get('skip_ret'):
                with contextlib.ExitStack() as ph:
                    qT = sb("qT", [128, NT], stack=ph)
                    kT = sb("kT", [128, NT], stack=ph)
                    vT = sb("vT", [128, NT], stack=ph)
                    gT = sb("gT", [128, NT], stack=ph)
                    oT = sb("oT", [128, NT], stack=ph)
                    ktf = sb("ktf", [128, 18, 128], stack=ph)
                    ktb = sb("ktb", [128, 18, 128], stack=ph)
                    vtok = sb("vtok", [128, 18, 128], stack=ph)
                    Sf = sb("Sf", [128, 18, 64], stack=ph)
                    Sb = sb("Sb", [128, 18, 64], stack=ph)
                    rt1 = sb("rt1", [128, 512], stack=ph)
                    rt2 = sb("rt2", [128, 512], stack=ph)
                    rt1b = sb("rt1b", [128, 512], stack=ph)
                    rt2b = sb("rt2b", [128, 512], stack=ph)
                    kdec = sb("kdec", [128, 4], stack=ph)
                    qfb = sb("qfb", [128, 2, 128], stack=ph)
                    mask = sb("mask", [128, 2, 128], stack=ph)
                    m2 = sb("m2", [128, 128], stack=ph)
                    gam = sb("gam", [128, 2], stack=ph)
                    sm = [sb(f"sm{i}", [128, 256], stack=ph) for i in range(2)]
                    qt = [sb(f"qt{i}", [128, 2, 128], stack=ph) for i in range(2)]
                    mixst = sb("mixst", [128, NT], BF16, stack=ph)
                    qz = [sb(f"qz{i}", [128, NT], stack=ph) for i in range(2)]
                    rope = sb("rope", [128, 2, NL], stack=ph)
                    S.dma('sp', rope[:], rope_d[:, :, :], w=['rope'], key='rope')
                    for p in range(dbg.get('ret_pairs', 4)):
                        for (t_, nm, ci) in ((qT, 'qT', p), (kT, 'kT', 4 + p), (vT, 'vT', 8 + p), (gT, 'gT', 12 + p)):
                            S.dma('sp', t_[:], p_d[ci, :, :], r=[f'p_d{ci}'], w=[nm], key=nm)
                        ts(S, kT[:], kT[:], 0.125, None, ALU.mult, r=['kT'], w=['kT'])
                        ropes = []
                        for si, (t_, nm, ra_, rb_) in enumerate(((qT, 'qT', rt1, rt2), (kT, 'kT', rt1b, rt2b))):
                            ropes.append(S.record(lambda k, si=si: (k + f'_{si}') if k in ('rt1', 'rt2') else k))
                            for i in range(4):
                                s0 = 256 + 512 * i
                                b = 4 * si + i
                                mm(S, PS[b][:, :], C('perm'), t_[:, s0:s0 + 512], r=[nm, 'cst'], w=[f'ps{b}'])
                                tt(S, ra_[:], t_[:, s0:s0 + 512], rope[:, 0, 512 * i:512 * (i + 1)], ALU.mult,
                                   r=[nm, 'rope'], w=['rt1'])
                                tt(S, rb_[:], PS[b][:, :], rope[:, 1, 512 * i:512 * (i + 1)], ALU.mult,
                                   r=[f'ps{b}', 'rope'], w=['rt2'])
                                tt(S, t_[:, s0:s0 + 512], ra_[:], rb_[:], ALU.add, r=['rt1', 'rt2'], w=[nm])
                            S.end_record()
                        S.merge_sched(ropes)
                        if dbg.get('ret_stage', 9) <= 1:
                            continue
                        lgr = PC[f'lgrep{l}']
                        act(S, kdec[:, 0:2], pv[:, lgr + 2 * p:lgr + 2 * p + 2], AF.Exp, scale=C('jrev', 1), r=['pv', 'cst'], w=['kdec'])
                        act(S, kdec[:, 2:4], pv[:, lgr + 8 + 2 * p:lgr + 8 + 2 * p + 2], AF.Exp, scale=C('jcol', 1), r=['pv', 'cst'], w=['kdec'])
                        act(S, qfb[:, 0, :], C('irow1'), AF.Exp, scale=P(f'lgcol{l}', p), r=['pv', 'cst'], w=['qfb'])
                        act(S, qfb[:, 1, :], C('irowb'), AF.Exp, scale=P(f'lgcol{l}', 4 + p), r=['pv', 'cst'], w=['qfb'])
                        act(S, gam[:, 0:1], P(f'lgcol{l}', p), AF.Exp, scale=128.0, r=['pv'], w=['gam'])
                        act(S, gam[:, 1:2], P(f'lgcol{l}', 4 + p), AF.Exp, scale=128.0, r=['pv'], w=['gam'])
                        for h in range(2):
                            act(S, mask[:, h, :], C('relpos'), AF.Exp, scale=pv[:, lgr + 2 * p + h:lgr + 2 * p + h + 1], r=['pv', 'cst'], w=['mask'])
                            tt(S, mask[:, h, :], mask[:, h, :], C('triu'), ALU.mult, r=['mask', 'cst'], w=['mask'])
                            act(S, m2[:], C('relneg'), AF.Exp, scale=pv[:, lgr + 8 + 2 * p + h:lgr + 8 + 2 * p + h + 1], r=['pv', 'cst'], w=['m2'])
                            tt(S, m2[:], m2[:], C('tril'), ALU.mult, r=['m2', 'cst'], w=['m2'])
                            tt(S, mask[:, h, :], mask[:, h, :], m2[:], ALU.add, r=['mask', 'm2'], w=['mask'])
                        if dbg.get('ret_stage', 9) <= 2:
                            continue
                        for c in range(18):
                            b = nextps()
                            tr(S, PS[b][:, 0:128], kT[:, c * 128:(c + 1) * 128], ident, r=['kT', 'cst'], w=[f'ps{b}'])
                            tr(S, PS[b][:, 128:256], vT[:, c * 128:(c + 1) * 128], ident, r=['vT', 'cst'], w=[f'ps{b}'])
                            kps = PS[b][:, 0:128].rearrange("p (h d) -> p h d", d=64)
                            tt(S, ktf[:, c, :].rearrange("p (h d) -> p h d", d=64), kps,
                               kdec[:, 0:2].rearrange("p (h o) -> p h o", o=1).to_broadcast([128, 2, 64]), ALU.mult,
                               r=[f'ps{b}', 'kdec'], w=['ktf'])
                            tt(S, ktb[:, c, :].rearrange("p (h d) -> p h d", d=64), kps,
                               kdec[:, 2:4].rearrange("p (h o) -> p h o", o=1).to_broadcast([128, 2, 64]), ALU.mult,
                               r=[f'ps{b}', 'kdec'], w=['ktb'])
                            cp(S, vtok[:, c, :], PS[b][:, 128:256], r=[f'ps{b}'], w=['vtok'], eng='act')
                        if dbg.get('ret_stage', 9) <= 3:
                            continue
                        mset(S, Sf[:, 0, :], 0.0, w=['Sf'])
                        mset(S, Sb[:, 1, :], 0.0, w=['Sb'])
                        of = list(range(18))
                        ob = [1, 0] + list(range(17, 1, -1))
                        chains = []
                        for (St, snm, kt, knm, order, gi) in ((Sf, 'Sf', ktf, 'ktf', of, 0), (Sb, 'Sb', ktb, 'ktb', ob, 1)):
                            chains.append(S.record(lambda k: k))
                            for i in range(17):
                                c, nx = order[i], order[i + 1]
                                b = 4 * gi + (i % 4)
                                for h in range(2):
                                    mm(S, PS[b][hsl(h), 0:64], kt[:, c, hsl(h)], vtok[:, c, hsl(h)], tp=(0, 64 * h),
                                       r=[knm, 'vtok'], w=[f'ps{b}'])
                                stt(S, St[:, nx, :], St[:, c, :], gam[:, gi:gi + 1], PS[b][:, 0:64], ALU.mult, ALU.add,
                                    r=[snm, 'gam', f'ps{b}'], w=[snm])
                            S.end_record()
                        S.merge(chains)
                        if dbg.get('ret_stage', 9) <= 4:
                            continue
                        for h in range(2):
                            mset(S, qz[h][hsl(1 - h), :], 0.0, w=[f'qz{h}'])
                            cp(S, qz[h][hsl(h), :], qT[hsl(h), :], r=['qT'], w=[f'qz{h}'], eng='act' if h else 'dve')
                        for c in range(2 if lastm else 0, 18):
                            cs_ = slice(c * 128, (c + 1) * 128)
                            bi = c % 2
                            b = nextps()
                            for h in range(2):
                                mm(S, PS[b][:, h * 128:(h + 1) * 128], kT[:, cs_], qz[h][:, cs_], r=['kT', f'qz{h}'], w=[f'ps{b}'])
                            tt(S, sm[bi][:], PS[b][:, 0:256], mask[:].rearrange("p h i -> p (h i)"), ALU.mult,
                               r=[f'ps{b}', 'mask'], w=[f'sm{bi}'])
                            tt(S, qt[bi][:], qfb[:], qT[:, cs_].rearrange("p (o i) -> p o i", o=1).to_broadcast([128, 2, 128]), ALU.mult,
                               r=['qfb', 'qT'], w=[f'qt{bi}'])
                            b2 = nextps()
                            for h in range(2):
                                mm(S, PS[b2][hsl(h), 0:128], vtok[:, c, hsl(h)], sm[bi][:, h * 128:(h + 1) * 128], start=True, stop=True,
                                   tp=(0, 64 * h), r=['vtok', f'sm{bi}'], w=[f'ps{b2}'])
                            b3 = nextps()
                            for h in range(2):
                                mm(S, PS[b3][hsl(h), 0:128], Sf[hsl(h), c, :], qt[bi][hsl(h), 0, :], start=True, stop=False,
                                   tp=(64 * h, 64 * h), r=['Sf', f'qt{bi}'], w=[f'ps{b3}'])
                                mm(S, PS[b3][hsl(h), 0:128], Sb[hsl(h), c, :], qt[bi][hsl(h), 1, :], start=False, stop=True,
                                   tp=(64 * h, 64 * h), r=['Sb', f'qt{bi}'], w=[f'ps{b3}'])
                            cp(S, oT[:, cs_], PS[b2][:, 0:128], r=[f'ps{b2}'], w=['oT'], eng='act')
                            tt(S, oT[:, cs_], oT[:, cs_], PS[b3][:, 0:128], ALU.add, r=['oT', f'ps{b3}'], w=['oT'])
                        if dbg.get('ret_stage', 9) <= 5:
                            continue
                        act(S, gT[:], gT[:], AF.Silu, r=['gT'], w=['gT'])
                        for (s0, n) in TBLK:
                            if lastm and s0 == 0:
                                continue
                            act(S, rt1[:, 0:n], oT[:, s0:s0 + n], AF.Square, r=['oT'], w=['rt1'])
                            b = nextps()
                            mm(S, PS[b][:, 0:n], C('bones'), rt1[:, 0:n], r=['rt1', 'cst'], w=[f'ps{b}'])
                            ts(S, rt2[:, 0:n], PS[b][:, 0:n], 1.0 / 64, EPS, ALU.mult, ALU.add, r=[f'ps{b}'], w=['rt2'])
                            act(S, rt2[:, 0:n], rt2[:, 0:n], AF.Sqrt, r=['rt2'], w=['rt2'])
                            recip(S, rt2[:, 0:n], rt2[:, 0:n], r=['rt2'], w=['rt2'])
                            tt(S, rt1[:, 0:n], oT[:, s0:s0 + n], rt2[:, 0:n], ALU.mult, r=['oT', 'rt2'], w=['rt1'])
                            stt(S, mixst[:, s0:s0 + n], rt1[:, 0:n], P(f'retnw{l}', p), gT[:, s0:s0 + n], ALU.mult, ALU.mult,
                                r=['rt1', 'pv', 'gT'], w=['mixst'])
                        lo = 256 if lastm else 0
                        S.dma('pool', mix_d[p, :, lo:NT], mixst[:, lo:NT], r=['mixst'], w=[f'mix_d{p}'], key='mixst')
                    S.flush('pd')

            if l == 1 and dbg.get('l1stop') == 'pd':
                break
            if not dbg.get('skip_rwkv'):
                rwkv_phase(nc, S, sb, PS, nextps, cst, pv, lora, CC, PC, C, P, p_d, mix_d, l, lastm, dbg)

            if dbg.get('stop') == 'mix' or (l == 1 and dbg.get('l1stop') == 'pe'):
                break
            with contextlib.ExitStack() as ph:
                wof = sb("wof", [128, 8, DM], stack=ph)
                wob = sb("wob", [128, 8, DM], BF16, stack=ph)
                mixb = [sb(f"mixb{i}", [128, 8, 512], BF16, stack=ph) for i in range(2)]
                xb = [sb(f"xb{i}", [128, 8, 512], stack=ph) for i in range(2)]
                xn = [sb(f"xn{i}", [128, 8, 512], stack=ph) for i in range(2)]
                S.dma('sp', wof[:, 0:4, :], wout_d[l, 0:512, :].rearrange("(kc p) n -> p kc n", p=128), w=['wof0'], key='wof0')
                S.dma('pool', wof[:, 4:8, :], wout_d[l, 512:1024, :].rearrange("(kc p) n -> p kc n", p=128), w=['wof1'], key='wof1')
                for kc in range(8):
                    cp(S, wob[:, kc, :], wof[:, kc, :], r=[f'wof{kc // 4}'], w=['wob'], eng='act' if kc % 2 else 'dve')
                if last:
                    sq = sb("sq", [128, 8, 512], stack=ph)
                    rstd = sb("rstd", [128, 512], stack=ph)
                    osb = [sb(f"osb{i}", [128, DM], stack=ph) for i in range(2)]
                ot = 0
                for bi_, (s0, n) in enumerate(TBLK):
                    if last and s0 == 0:
                        continue
                    bi = bi_ % 2
                    si = 1 if s0 == 0 else 0
                    S.dma('sp', mixb[bi][:, :, 0:n], mix_d[:, :, s0:s0 + n].rearrange("c p t -> p c t"),
                          r=[f'mix_d{i}' for i in range(8)], w=[f'mixb{bi}'], key=f'mixb{bi}')
                    S.dma('sp', xb[bi][:, :, 0:n], xt_d[:, :, s0:s0 + n], r=['xt_d'], w=[f'xb{bi}'], key=f'xb{bi}')
                    for dm in range(8):
                        b = nextps()
                        for kc in range(8):
                            mm(S, PS[b][:, 0:n], wob[:, kc, dm * 128:(dm + 1) * 128], mixb[bi][:, kc, 0:n], start=(kc == 0), stop=(kc == 7),
                               r=['wob', f'mixb{bi}'], w=[f'ps{b}'])
                        stt(S, xn[bi][:, dm, 0:n], PS[b][:, 0:n], mod[:, l, 16 + dm, si:si + 1], xb[bi][:, dm, 0:n], ALU.mult, ALU.add,
                            r=[f'ps{b}', 'mod', f'xb{bi}'], w=[f'xn{bi}'])
                    if not last:
                        S.dma('pool', xt_d[:, :, s0:s0 + n], xn[bi][:, :, 0:n], r=[f'xn{bi}'], w=['xt_d'], key=f'xn{bi}')
                    else:
                        act(S, sq[:, :, 0:n], xn[bi][:, :, 0:n], AF.Square, r=[f'xn{bi}'], w=['sq'])
                        b = nextps()
                        for kc in range(8):
                            mm(S, PS[b][:, 0:n], C('ones'), sq[:, kc, 0:n], start=(kc == 0), stop=(kc == 7), r=['sq', 'cst'], w=[f'ps{b}'])
                        ts(S, rstd[:, 0:n], PS[b][:, 0:n], 1.0 / DM, EPS, ALU.mult, ALU.add, r=[f'ps{b}'], w=['rstd'])
                        act(S, rstd[:, 0:n], rstd[:, 0:n], AF.Sqrt, r=['rstd'], w=['rstd'])
                        recip(S, rstd[:, 0:n], rstd[:, 0:n], r=['rstd'], w=['rstd'])
                        for dm in range(8):
                            stt(S, xn[bi][:, dm, 0:n], xn[bi][:, dm, 0:n], P('fnw', dm), rstd[:, 0:n], ALU.mult, ALU.mult,
                                r=[f'xn{bi}', 'pv', 'rstd'], w=[f'xn{bi}'])
                        for t in range(n // 128):
                            oi = ot % 2
                            ot += 1
                            for half in range(2):
                                b = nextps()
                                for k in range(4):
                                    dm = half * 4 + k
                                    tr(S, PS[b][:, k * 128:(k + 1) * 128], xn[bi][:, dm, t * 128:(t + 1) * 128], ident,
                                       r=[f'xn{bi}', 'cst'], w=[f'ps{b}'])
                                cp(S, osb[oi][:, half * 512:(half + 1) * 512], PS[b][:, :], r=[f'ps{b}'], w=[f'osb{oi}'],
                                   eng='act' if half else 'dve')
                            r0 = s0 - 256 + t * 128
                            S.dma('pool', out_d[r0:r0 + 128, :], osb[oi][:], r=[f'osb{oi}'], w=['out_d'], key=f'osb{oi}')
                S.flush('pf')
    return nc


def rwkv_phase(nc, S, sb_, PS, nextps, cst, pv, lora, CC, PC, C, P, p_d, mix_d, l, last, dbg):
    with contextlib.ExitStack() as ph:
        def sb(name, shape, dt=F32):
            return sb_(name, shape, dt, stack=ph)

        ident = C('ident')
        raw = sb("raw", [128, NT])
        xwxa = sb("xwxa", [128, NT])
        rT = sb("rT", [128, NT])
        kT = sb("kT", [128, NT])
        vT = sb("vT", [128, NT])
        gT = sb("gT", [128, NT])
        kkT = sb("kkT", [128, NT])
        bonus = sb("bonus", [128, NT])
        YT = sb("YT", [128, NT])
        t5a = sb("t5a", [128, 512])
        t5b = sb("t5b", [128, 512])
        omka = sb("omka", [128, 1])
        mixst = sb("mixst", [128, NT], BF16)
        names = ['lw', 'aa', 'km', 'bb', 'kh', 'bh', 'Kt', 'Bt']
        SB_ = []
        for d_ in range(2):
            bset = dict(
                T_=dict({n_: sb(f"bt{d_}_" + n_, [128, 256]) for n_ in names}, EA=sb(f"EA{d_}", [128, 4, 256])),
                kr=sb(f"kr{d_}", [128, 4, 2, 64]), tot=sb(f"tot{d_}", [128, 4]), pcx=sb(f"pcx{d_}", [128, 4]),
                tokm=sb(f"tokm{d_}", [128, 4, 256]), AB=sb(f"AB{d_}", [128, 4, 256]), Nk=sb(f"Nk{d_}", [128, 2, 256]),
                Q=sb(f"Q{d_}", [128, 256]), Z=sb(f"Z{d_}", [128, 256]), WT=sb(f"WT{d_}", [128, 256]),
                Tst=sb(f"Tst{d_}", [128, 64]), Un=sb(f"Un{d_}", [128, 64]))
            SB_.append(bset)
        YT1 = sb("YT1", [128, NT])
        YTs = [YT, YT1]
        SHARED = {'cst', 'pv', 'lora', 'xwxa', 'rT', 'kT', 'vT', 'kkT', 'omka'}

        def conv(out, onm, j, src=None, snm='raw'):
            src = raw if src is None else src
            base = PC[f'shw{l}']
            w0 = pv[:, base + 0 * 13 + j:base + 0 * 13 + j + 1]
            w1 = pv[:, base + 1 * 13 + j:base + 1 * 13 + j + 1]
            w2 = pv[:, base + 2 * 13 + j:base + 2 * 13 + j + 1]
            ts(S, out[:], src[:], w1, None, ALU.mult, r=[snm, 'pv'], w=[onm])
            for (a0, a1) in ((0, 256), (256, NT)):
                stt(S, out[:, a0 + 1:a1], src[:, a0:a1 - 1], w0, out[:, a0 + 1:a1], ALU.mult, ALU.add, r=[snm, 'pv', onm], w=[onm])
                stt(S, out[:, a0:a1 - 1], src[:, a0 + 1:a1], w2, out[:, a0:a1 - 1], ALU.mult, ALU.add, r=[snm, 'pv', onm], w=[onm])

        S.dma('sp', raw[:], p_d[28, :, :], r=['p_d28'], w=['raw'], key='raw')
        conv(xwxa, 'xwxa', 12)
        act(S, xwxa[0:64, :], xwxa[0:64, :], AF.Tanh, r=['xwxa'], w=['xwxa'])

        if dbg.get('rwkv_stage', 99) <= 1:
            S.flush('pe')
            return
        for p in range(dbg.get('rwkv_pairs', 4)):
            S.dma('sp', raw[:], p_d[16 + p, :, :], r=[f'p_d{16 + p}'], w=['raw'], key='raw')
            S.dma('sp', YT1[:], p_d[20 + p, :, :], r=[f'p_d{20 + p}'], w=['YT_1'], key='yt1raw')
            conv(rT, 'rT', p)
            S.dma('sp', raw[:], p_d[24 + p, :, :], r=[f'p_d{24 + p}'], w=['raw'], key='raw')
            conv(kT, 'kT', 4 + p, src=YT1, snm='YT_1')
            conv(vT, 'vT', 8 + p)
            S.dma('sp', gT[:], p_d[29 + p, :, :], r=[f'p_d{29 + p}'], w=['gT'], key='gT')
            act(S, gT[:], gT[:], AF.Silu, r=['gT'], w=['gT'])
            ts(S, omka[:], P(f'ka{l}', p), -1.0, 1.0, ALU.mult, ALU.add, r=['pv'], w=['omka'])
            ts(S, kkT[:], kT[:], P(f'kk{l}', p), None, ALU.mult, r=['kT', 'pv'], w=['kkT'])
            stt(S, bonus[:], rT[:], P(f'rk{l}', p), kT[:], ALU.mult, ALU.mult, r=['rT', 'kT', 'pv'], w=['bonus'])
            for (s0, n) in TBLK:
                act(S, t5a[:, 0:n], kkT[:, s0:s0 + n], AF.Square, r=['kkT'], w=['t5a'])
                b = nextps()
                mm(S, PS[b][:, 0:n], C('bones'), t5a[:, 0:n], r=['t5a', 'cst'], w=[f'ps{b}'])
                act(S, t5b[:, 0:n], PS[b][:, 0:n], AF.Sqrt, r=[f'ps{b}'], w=['t5b'])
                ts(S, t5b[:, 0:n], t5b[:, 0:n], 1e-12, None, ALU.max, r=['t5b'], w=['t5b'])
                recip(S, t5b[:, 0:n], t5b[:, 0:n], r=['t5b'], w=['t5b'])
                tt(S, kkT[:, s0:s0 + n], kkT[:, s0:s0 + n], t5b[:, 0:n], ALU.mult, r=['kkT', 't5b'], w=['kkT'])
                b = nextps()
                mm(S, PS[b][:, 0:n], C('bones'), bonus[:, s0:s0 + n], r=['bonus', 'cst'], w=[f'ps{b}'])
                tt(S, bonus[:, s0:s0 + n], PS[b][:, 0:n], vT[:, s0:s0 + n], ALU.mult, r=[f'ps{b}', 'vT'], w=['bonus'])
            streams = []
            for d in range(2 if dbg.get('rwkv_stage', 99) > 2 else 0):
                sfx = f"_{d}"
                rec = S.record(lambda k, sfx=sfx: k if (k in SHARED or k.startswith('ps')) else k + sfx)
                bn = [0]

                def nps(d=d, bn=bn):
                    bn[0] = (bn[0] + 1) % 4
                    return 4 * d + bn[0]
                B_ = SB_[d]
                mset(S, YTs[d][:], 0.0, w=['YT'])
                mset(S, B_['Tst'][:], 0.0, w=['Tst'])
                border = list(range(9)) if d == 0 else [0] + list(range(8, 0, -1))
                border = border[:dbg.get('rwkv_blocks', 9)]
                for blk in border:
                    rwkv_block(S, PS, nps, cst, pv, lora, CC, PC, C, P, l, p, d, blk,
                               rT, kT, vT, kkT, xwxa, YTs[d], B_['T_'], B_['kr'], B_['tot'], B_['pcx'], B_['tokm'], B_['AB'],
                               B_['Nk'], B_['Q'], B_['Z'], B_['WT'], B_['Tst'], B_['Un'], omka)
                S.end_record()
                streams.append(rec)
            if dbg.get('rr_merge'):
                S.merge(streams, [0, dbg.get('merge_off', 0)][:len(streams)])
            else:
                S.merge_sched(streams)
            for (s0, n) in TBLK:
                tt(S, YT[:, s0:s0 + n], YT[:, s0:s0 + n], YT1[:, s0:s0 + n], ALU.add, r=['YT_0', 'YT_1'], w=['YT_0'])

            for (s0, n) in TBLK:
                if last and s0 == 0:
                    continue
                b = nextps()
                mm(S, PS[b][:, 0:n], C('bones'), YT[:, s0:s0 + n], r=['YT_0', 'cst'], w=[f'ps{b}'])
                stt(S, t5a[:, 0:n], PS[b][:, 0:n], -1.0 / 64, YT[:, s0:s0 + n], ALU.mult, ALU.add, r=[f'ps{b}', 'YT_0'], w=['t5a'])
                act(S, t5b[:, 0:n], t5a[:, 0:n], AF.Square, r=['t5a'], w=['t5b'])
                b = nextps()
                mm(S, PS[b][:, 0:n], C('bones'), t5b[:, 0:n], r=['t5b', 'cst'], w=[f'ps{b}'])
                ts(S, t5b[:, 0:n], PS[b][:, 0:n], 1.0 / 64, GN_EPS, ALU.mult, ALU.add, r=[f'ps{b}'], w=['t5b'])
                act(S, t5b[:, 0:n], t5b[:, 0:n], AF.Sqrt, r=['t5b'], w=['t5b'])
                recip(S, t5b[:, 0:n], t5b[:, 0:n], r=['t5b'], w=['t5b'])
                tt(S, t5a[:, 0:n], t5a[:, 0:n], t5b[:, 0:n], ALU.mult, r=['t5a', 't5b'], w=['t5a'])
                ts(S, t5a[:, 0:n], t5a[:, 0:n], P(f'lnw{l}', p), P(f'lnb{l}', p), ALU.mult, ALU.add, r=['t5a', 'pv'], w=['t5a'])
                tt(S, t5a[:, 0:n], t5a[:, 0:n], bonus[:, s0:s0 + n], ALU.add, r=['t5a', 'bonus'], w=['t5a'])
                tt(S, mixst[:, s0:s0 + n], t5a[:, 0:n], gT[:, s0:s0 + n], ALU.mult, r=['t5a', 'gT'], w=['mixst'])
            lo = 256 if last else 0
            S.dma('pool', mix_d[4 + p, :, lo:NT], mixst[:, lo:NT], r=['mixst'], w=[f'mix_d{4 + p}'], key='mixst')
            if dbg.get('rwkv_dump') and p == 0:
                for i_, (t_, nm_) in enumerate(((rT, 'rT'), (kT, 'kT'), (vT, 'vT'), (kkT, 'kkT'), (bonus, 'bonus'), (YT, 'YT_0'), (xwxa, 'xwxa'))):
                    S.dma('pool', p_d[i_, :, :], t_[:], r=[nm_], w=[f'p_d{i_}'], key=nm_)
        S.flush('pe')


def rwkv_block(S, PS, nextps, cst, pv, lora, CC, PC, C, P, l, p, d, blk,
               rT, kT, vT, kkT, xwxa, YT, T_, kr, tot, pcx, tokm, AB, Nk, Q, Z, WT, Tst, Un, omka):
    s0 = blk * 256
    bs = slice(s0, s0 + 256)
    lw, aa, km, bb, kh, bh, Kt, Bt = [T_[n_] for n_ in ['lw', 'aa', 'km', 'bb', 'kh', 'bh', 'Kt', 'Bt']]
    EA = T_['EA']

    class _V:
        def __getitem__(self, k):
            return EA[:, 0, :]
    LL = _V()
    ld = l * 2 + d
    b = nextps()
    mm(S, PS[b][:, 0:256], lora[:, ld, p * 128:(p + 1) * 128], xwxa[:, bs], r=['lora', 'xwxa'], w=[f'ps{b}'])
    b2 = nextps()
    mm(S, PS[b2][:, 0:256], lora[:, 4 + ld, p * 128:(p + 1) * 128], xwxa[:, bs], r=['lora', 'xwxa'], w=[f'ps{b2}'])
    act(S, lw[:], PS[b][:, 0:256], AF.Sigmoid, bias=P(f'w0{l}', d * 4 + p), r=[f'ps{b}', 'pv'], w=['lw'])
    ts(S, lw[:], lw[:], -0.6065306597126334, None, ALU.mult, r=['lw'], w=['lw'])
    act(S, aa[:], PS[b2][:, 0:256], AF.Sigmoid, bias=P(f'a0{l}', d * 4 + p), r=[f'ps{b2}', 'pv'], w=['aa'])
    ts(S, km[:], aa[:], P(f'ka{l}', p), omka[:, 0:1], ALU.mult, ALU.add, r=['aa', 'pv', 'omka'], w=['km'])
    tt(S, km[:], km[:], kT[:, bs], ALU.mult, r=['km', 'kT'], w=['km'])
    tt(S, bb[:], kkT[:, bs], aa[:], ALU.mult, r=['kkT', 'aa'], w=['bb'])
    if DEBUG.get('rwkv_stage', 99) <= 3:
        return
    S.op('dve', lambda e: e.tensor_tensor_scan(out=LL[:], data0=C('cmask', 256), data1=lw[:], initial=0.0,
                                               op0=ALU.mult, op1=ALU.add), ['cst', 'lw'], ['EA'], sync_same=True)
    L3 = LL[:].rearrange("p (c t) -> p c t", t=64)
    cp(S, tot[:], LL[:].rearrange("p (c t) -> p c t", t=64)[:, :, 63], r=['EA'], w=['tot'])
    totb = tot[:].rearrange("p (c o) -> p c o", o=1).to_broadcast([128, 4, 64])
    if d == 1:
        tt(S, L3, totb, L3, ALU.subtract, r=['tot', 'EA'], w=['EA'])
        tt(S, LL[:], LL[:], lw[:], ALU.add, r=['EA', 'lw'], w=['EA'])
    krv = kr[:]
    tt(S, EA[:, 1, :], LL[:], lw[:], ALU.subtract, r=['EA', 'lw'], w=['EA'])
    ts(S, EA[:, 2, :], LL[:], -1.0, None, ALU.mult, r=['EA'], w=['EA'])
    tt(S, EA[:, 3, :].rearrange("p (c t) -> p c t", t=64), totb, L3, ALU.subtract, r=['tot', 'EA'], w=['EA'])
    act(S, EA[:].rearrange("p a x -> p (a x)"), EA[:].rearrange("p a x -> p (a x)"), AF.Exp, r=['EA'], w=['EA'])
    act(S, pcx[:], tot[:], AF.Exp, r=['tot'], w=['pcx'])
    v3 = lambda ap: ap.rearrange("p (c t) -> p c t", t=64)
    tt(S, krv[:, :, 1, :], v3(rT[:, bs]), v3(EA[:, 0, :]), ALU.mult, r=['rT', 'EA'], w=['kr'])
    tt(S, krv[:, :, 0, :], v3(kkT[:, bs]), v3(EA[:, 1, :]), ALU.mult, r=['kkT', 'EA'], w=['kr'])
    tt(S, kh[:], km[:], EA[:, 2, :], ALU.mult, r=['km', 'EA'], w=['kh'])
    tt(S, bh[:], bb[:], EA[:, 2, :], ALU.mult, r=['bb', 'EA'], w=['bh'])
    tt(S, Kt[:], km[:], EA[:, 3, :], ALU.mult, r=['km', 'EA'], w=['Kt'])
    tt(S, Bt[:], bb[:], EA[:, 3, :], ALU.mult, r=['bb', 'EA'], w=['Bt'])
    if DEBUG.get('rwkv_stage', 99) <= 4:
        return
    srcs = [(None, 'kr'), (Kt[:], 'Kt'), (Bt[:], 'Bt'), (vT[:, bs], 'vT')]
    for half in range(2):
        b = nextps()
        for kk_ in range(2):
            kind = half * 2 + kk_
            src, snm = srcs[kind]
            for c in range(4):
                for h in range(2):
                    lh = krv[hsl(h), c, 0, :] if src is None else src[hsl(h), c * 64:(c + 1) * 64]
                    mm(S, PS[b][hsl(h), kk_ * 256 + c * 64:kk_ * 256 + (c + 1) * 64], lh,
                       cst[hsl(h), CC['ident'] + 64 * h:CC['ident'] + 64 * h + 64], tp=(64 * h, 64 * h),
                       r=[snm, 'cst'], w=[f'ps{b}'])
        cp(S, tokm[:, half * 2:half * 2 + 2, :], PS[b][:, :].rearrange("p (k x) -> p k x", x=256), r=[f'ps{b}'], w=['tokm'],
           eng='act' if half else 'dve')
    if DEBUG.get('rwkv_stage', 99) <= 5:
        return
    bA = [nextps(), nextps()]
    for c in range(4):
        pb = bA[c // 2]
        o0 = (c % 2) * 256
        for h in range(2):
            rhs = krv[hsl(h), c, :, :].rearrange("p a t -> p (a t)")
            mm(S, PS[pb][hsl(h), o0:o0 + 128], kh[hsl(h), c * 64:(c + 1) * 64], rhs, tp=(64 * h, 64 * h), r=['kh', 'kr'], w=[f'ps{pb}'])
            mm(S, PS[pb][hsl(h), o0 + 128:o0 + 256], bh[hsl(h), c * 64:(c + 1) * 64], rhs, tp=(64 * h, 64 * h), r=['bh', 'kr'], w=[f'ps{pb}'])
    mab = cst[:, CC[f'mab{d}']:CC[f'mab{d}'] + 256]
    for i2 in range(2):
        tt(S, AB[:, 2 * i2:2 * i2 + 2, :], PS[bA[i2]][:, :].rearrange("p (c x) -> p c x", x=256),
           mab.rearrange("p (o x) -> p o x", o=1).to_broadcast([128, 2, 256]), ALU.mult, r=[f'ps{bA[i2]}', 'cst'], w=['AB'])
    b = nextps()
    for c in range(4):
        for h in range(2):
            mm(S, PS[b][hsl(h), c * 64:(c + 1) * 64], krv[hsl(h), c, 0, :], bh[hsl(h), c * 64:(c + 1) * 64], tp=(64 * h, 64 * h),
               r=['kr', 'bh'], w=[f'ps{b}'])
    tt(S, Nk[:, 1, :], PS[b][:, 0:256], cst[:, CC[f'mn{d}']:CC[f'mn{d}'] + 256], ALU.mult, r=[f'ps{b}', 'cst'], w=['Nk'])
    if DEBUG.get('rwkv_stage', 99) <= 6:
        return
    AbT = AB[:, :, 128:192]
    cp(S, Nk[:, 0, :].rearrange("p (c t) -> p c t", t=64), AbT, r=['AB'], w=['Nk'], eng='act')
    tt(S, Q[:].rearrange("p (c t) -> p c t", t=64), C('identrep', 256).rearrange("p (c t) -> p c t", t=64), AbT, ALU.subtract,
       r=['cst', 'AB'], w=['Q'])
    for it in range(5):
        b = nextps()
        for c in range(4):
            cs_ = slice(c * 64, (c + 1) * 64)
            for h in range(2):
                if it < 4:
                    mm(S, PS[b][hsl(h), c * 64:(c + 1) * 64], Nk[hsl(h), 1, cs_], Nk[hsl(h), 0, cs_], tp=(64 * h, 64 * h),
                       r=['Nk'], w=[f'ps{b}'])
                mm(S, PS[b][hsl(h), 256 + c * 64:256 + (c + 1) * 64], Nk[hsl(h), 0, cs_], Nk[hsl(h), 1, cs_], tp=(64 * h, 64 * h),
                   r=['Nk'], w=[f'ps{b}'])
        if it < 4:
            cp(S, Nk[:].rearrange("p a x -> p (a x)"), PS[b][:, :], r=[f'ps{b}'], w=['Nk'])
        else:
            cp(S, Nk[:, 1, :], PS[b][:, 256:512], r=[f'ps{b}'], w=['Nk'])
        b2 = nextps()
        for c in range(4):
            cs_ = slice(c * 64, (c + 1) * 64)
            for h in range(2):
                mm(S, PS[b2][hsl(h), cs_], Nk[hsl(h), 1, cs_], Q[hsl(h), cs_], tp=(64 * h, 64 * h), r=['Nk', 'Q'], w=[f'ps{b2}'])
        tt(S, Q[:], Q[:], PS[b2][:, 0:256], ALU.add, r=['Q', f'ps{b2}'], w=['Q'])
    if DEBUG.get('rwkv_stage', 99) <= 7:
        return
    b = nextps()
    for c in range(4):
        cs_ = slice(c * 64, (c + 1) * 64)
        for h in range(2):
            mm(S, PS[b][hsl(h), cs_], AB[hsl(h), c, 0:64], tokm[hsl(h), 3, cs_], tp=(64 * h, 64 * h), r=['AB', 'tokm'], w=[f'ps{b}'])
    cp(S, Z[:], PS[b][:, 0:256], r=[f'ps{b}'], w=['Z'], eng='act')
    b = nextps()
    for c in range(4):
        cs_ = slice(c * 64, (c + 1) * 64)
        for h in range(2):
            mm(S, PS[b][hsl(h), cs_], tokm[hsl(h), 0, cs_], Q[hsl(h), cs_], tp=(64 * h, 64 * h), r=['tokm', 'Q'], w=[f'ps{b}'])
    cp(S, WT[:], PS[b][:, 0:256], r=[f'ps{b}'], w=['WT'])
    if DEBUG.get('rwkv_stage', 99) <= 8:
        return
    corder = range(4) if d == 0 else range(3, -1, -1)
    for c in corder:
        cs_ = slice(c * 64, (c + 1) * 64)
        b = nextps()
        for h in range(2):
            mm(S, PS[b][hsl(h), 0:64], Q[hsl(h), cs_], Z[hsl(h), cs_], start=True, stop=False, tp=(64 * h, 64 * h), r=['Q', 'Z'], w=[f'ps{b}'])
            mm(S, PS[b][hsl(h), 0:64], WT[hsl(h), cs_], Tst[hsl(h), :], start=False, stop=True, tp=(64 * h, 64 * h),
               r=['WT', 'Tst'], w=[f'ps{b}'])
        ts(S, Un[:], PS[b][:, 0:64], -1.0, None, ALU.mult, r=[f'ps{b}'], w=['Un'])
        b2 = nextps()
        for h in range(2):
            mm(S, PS[b2][hsl(h), 0:64], Tst[hsl(h), :], kr[hsl(h), c, 1, :], start=True, stop=False, tp=(64 * h, 64 * h),
               r=['Tst', 'kr'], w=[f'ps{b2}'])
            mm(S, PS[b2][hsl(h), 0:64], tokm[hsl(h), 3, cs_], AB[hsl(h), c, 64:128], start=False, stop=False, tp=(64 * h, 64 * h),
               r=['tokm', 'AB'], w=[f'ps{b2}'])
            mm(S, PS[b2][hsl(h), 0:64], Un[hsl(h), :], AB[hsl(h), c, 192:256], start=False, stop=True, tp=(64 * h, 64 * h),
               r=['Un', 'AB'], w=[f'ps{b2}'])
        b3 = nextps()
        for h in range(2):
            mm(S, PS[b3][hsl(h), 0:64], tokm[hsl(h), 1, cs_], tokm[hsl(h), 3, cs_], start=True, stop=False, tp=(64 * h, 64 * h),
               r=['tokm'], w=[f'ps{b3}'])
            mm(S, PS[b3][hsl(h), 0:64], tokm[hsl(h), 2, cs_], Un[hsl(h), :], start=False, stop=True, tp=(64 * h, 64 * h),
               r=['tokm', 'Un'], w=[f'ps{b3}'])
        stt(S, Tst[:], Tst[:], pcx[:, c:c + 1], PS[b3][:, 0:64], ALU.mult, ALU.add, r=['Tst', 'pcx', f'ps{b3}'], w=['Tst'])
        ys = slice(s0 + c * 64, s0 + (c + 1) * 64)
        tt(S, YT[:, ys], YT[:, ys], PS[b2][:, 0:64], ALU.add, r=['YT', f'ps{b2}'], w=['YT'])


_CACHE = {}


def kernel(x, c, ctx, c_ctx, norm_w, w_mod, b_mod, w_in, ret_log_gamma, ret_norm_w,
           rwkv_shift_w, rwkv_w0, rwkv_w2, rwkv_a0, rwkv_a2, rwkv_k_k, rwkv_k_a, rwkv_r_k,
           rwkv_ln_w, rwkv_ln_b, w_out, final_norm_w):
    inp = dict(x=x, c=c, ctx=ctx, c_ctx=c_ctx, norm_w=norm_w, w_mod=w_mod, b_mod=b_mod, w_in=w_in,
               ret_log_gamma=ret_log_gamma, ret_norm_w=ret_norm_w, rwkv_shift_w=rwkv_shift_w, rwkv_w0=rwkv_w0,
               rwkv_w2=rwkv_w2, rwkv_a0=rwkv_a0, rwkv_a2=rwkv_a2, rwkv_k_k=rwkv_k_k, rwkv_k_a=rwkv_k_a,
               rwkv_r_k=rwkv_r_k, rwkv_ln_w=rwkv_ln_w, rwkv_ln_b=rwkv_ln_b, w_out=w_out, final_norm_w=final_norm_w)
    inp = {k: np.asarray(v, dtype=np.float32) for k, v in inp.items()}
    cst, CC, rope = _consts()
    pvs = [_pvec(inp, b) for b in range(8)]
    PC = pvs[0][1]
    zz = np.zeros_like(inp['rwkv_w2'])
    lw_ = np.concatenate([inp['rwkv_w2'], zz], axis=2).transpose(2, 0, 1, 3).reshape(128, 4, 512)
    la_ = np.concatenate([zz, inp['rwkv_a2']], axis=2).transpose(2, 0, 1, 3).reshape(128, 4, 512)
    lora = np.ascontiguousarray(np.concatenate([lw_, la_], axis=1))
    nc = build(CC, PC, cst.shape[1], pvs[0][0].shape[1], DEBUG)
    in_maps = []
    for b in range(8):
        in_maps.append({
            "x": np.ascontiguousarray(inp['x'][b]), "ctx": np.ascontiguousarray(inp['ctx'][b]),
            "cst": cst, "pv": pvs[b][0], "lora": lora, "rope": rope,
            "w_mod": inp['w_mod'], "w_in": inp['w_in'], "w_out": inp['w_out'],
        })
    res = run_bass_kernel_spmd(nc, in_maps, core_ids=list(range(8)))
    _CACHE['res'] = res
    return np.stack([np.asarray(r["out"], dtype=np.float32) for r in res.results], axis=0)
```
